# Optimizing a Trainium2 kernel written in Bass

```python
import math
import jax, jax.numpy as jnp
from jax import lax
import numpy as np

D_MODEL = 1024
BATCH = 8
SEQ = 2048
DEPTH = 1
DEC_BATCH = 128
DEC_SEQ = 8
PAST_LEN = 16384
PAGE_SIZE = 128

CHUNK = 128
W_A = D_MODEL
N_GROUPS_A = 4
GROUP_A = W_A // N_GROUPS_A
W_B = D_MODEL
CONV_W = 31
D_FF = ((8 * D_MODEL // 3 + 255) // 256) * 256
IN_W = 2 * W_A + 2 * W_B + 2 * D_MODEL
EPS = 1e-6

kernel_name = "gated_gmlp_conformer_hybrid_step"


def rmsnorm(x, g):
    xf = x.astype(jnp.float32)
    y = xf * lax.rsqrt(jnp.mean(xf * xf, axis=-1, keepdims=True) + EPS)
    return (y * g.astype(jnp.float32)).astype(x.dtype)


def layernorm(x, g, b):
    xf = x.astype(jnp.float32)
    mu = jnp.mean(xf, axis=-1, keepdims=True)
    var = jnp.mean(jnp.square(xf - mu), axis=-1, keepdims=True)
    y = (xf - mu) * lax.rsqrt(var + EPS)
    return (y * g.astype(jnp.float32) + b.astype(jnp.float32)).astype(x.dtype)


def spatial_gating(u, v, w_s, b_s):
    B, S, _ = v.shape
    L = min(CHUNK, S)
    n_chunks = S // L
    vr = v.reshape(B, n_chunks, L, N_GROUPS_A, GROUP_A)
    mask = jnp.tril(jnp.ones((L, L), dtype=bool))
    w = jnp.where(mask[None], w_s[:, :L, :L], jnp.zeros((), w_s.dtype))
    mixed = jnp.einsum('gts,bcsgd->bctgd', w, vr)
    mixed = mixed + jnp.transpose(b_s[:, :L])[None, None, :, :, None]
    return u * mixed.reshape(B, S, W_A)


def causal_depthwise_conv(h, buf, conv_w, conv_b):
    full = jnp.concatenate([buf, h], axis=1)
    new_buf = full[:, -(CONV_W - 1):]
    out = lax.conv_general_dilated(
        full, conv_w[:, None, :], window_strides=(1,), padding='VALID',
        dimension_numbers=('NWC', 'WIO', 'NWC'), feature_group_count=W_B)
    return out + conv_b, new_buf


def mixer(h, conv_buf, w_in, b_in, ln_v_g, ln_v_b, w_s, b_s, w_pa, b_pa,
          conv_w, conv_b, ln_c_g, ln_c_b, w_pb, b_pb, w_o):
    S = h.shape[1]
    proj = h @ w_in + b_in
    z, glu_in, gates = jnp.split(proj, [2 * W_A, 2 * W_A + 2 * W_B], axis=-1)
    z = jax.nn.gelu(z)
    u, v = jnp.split(z, 2, axis=-1)
    v = layernorm(v, ln_v_g, ln_v_b)
    a = spatial_gating(u, v, w_s, b_s) @ w_pa + b_pa
    last_start = ((S - 1) // CHUNK) * CHUNK
    chunk_v = v[:, last_start:]
    ga, gb = jnp.split(glu_in, 2, axis=-1)
    glu = ga * jax.nn.sigmoid(gb)
    if conv_buf is None:
        conv_buf = jnp.zeros((h.shape[0], CONV_W - 1, W_B), glu.dtype)
    cv, new_buf = causal_depthwise_conv(glu, conv_buf, conv_w, conv_b)
    bb = jax.nn.silu(layernorm(cv, ln_c_g, ln_c_b)) @ w_pb + b_pb
    g_a, g_b = jnp.split(gates, 2, axis=-1)
    merged = jax.nn.sigmoid(g_a) * a + jax.nn.sigmoid(g_b) * bb
    return merged @ w_o, new_buf, chunk_v


def layer(x, c, conv_buf, w_ada, b_ada, g_norm1, w_in, b_in, ln_v_g, ln_v_b, w_s, b_s,
          w_pa, b_pa, conv_w, conv_b, ln_c_g, ln_c_b, w_pb, b_pb, w_o,
          g_norm2, w_ffn_in, w_ffn_out):
    mod = jax.nn.silu(c) @ w_ada + b_ada
    sh1, sc1, gt1, sh2, sc2, gt2 = [m[:, None, :] for m in jnp.split(mod, 6, axis=-1)]
    h = rmsnorm(x, g_norm1) * (1.0 + sc1) + sh1
    mix, new_buf, chunk_v = mixer(h, conv_buf, w_in, b_in, ln_v_g, ln_v_b, w_s, b_s, w_pa, b_pa,
                                  conv_w, conv_b, ln_c_g, ln_c_b, w_pb, b_pb, w_o)
    x = x + gt1 * mix
    h = rmsnorm(x, g_norm2) * (1.0 + sc2) + sh2
    gate, up = jnp.split(h @ w_ffn_in, 2, axis=-1)
    x = x + gt2 * ((jax.nn.silu(gate) * up) @ w_ffn_out)
    return x, new_buf, chunk_v


def setup_inputs(seed: int = 0) -> dict:
    key = jax.random.key(seed)
    ks = iter(jax.random.split(key, 40))
    f32 = jnp.float32

    def nrm(shape, scale):
        return jax.random.normal(next(ks), shape, f32) * scale

    def gain(shape):
        return 1.0 + 0.05 * jax.random.normal(next(ks), shape, f32)

    D = D_MODEL
    return {
        "x_prompt": nrm((BATCH, SEQ, D), 1.0),
        "x_sample": nrm((DEC_BATCH, DEC_SEQ, D), 1.0),
        "c_prompt": nrm((BATCH, D), 1.0),
        "c_sample": nrm((DEC_BATCH, D), 1.0),
        "cache_conv": nrm((DEPTH, DEC_BATCH, CONV_W - 1, W_B), 0.5),
        "w_ada": nrm((DEPTH, D, 6 * D), 0.2 * D ** -0.5),
        "b_ada": nrm((DEPTH, 6 * D), 0.02),
        "g_norm1": gain((DEPTH, D)),
        "w_in": nrm((DEPTH, D, IN_W), D ** -0.5),
        "b_in": nrm((DEPTH, IN_W), 0.02),
        "ln_v_g": gain((DEPTH, W_A)),
        "ln_v_b": nrm((DEPTH, W_A), 0.02),
        "w_s": nrm((DEPTH, N_GROUPS_A, CHUNK, CHUNK), CHUNK ** -0.5),
        "b_s": gain((DEPTH, N_GROUPS_A, CHUNK)),
        "w_pa": nrm((DEPTH, W_A, D), W_A ** -0.5),
        "b_pa": nrm((DEPTH, D), 0.02),
        "conv_w": nrm((DEPTH, CONV_W, W_B), CONV_W ** -0.5),
        "conv_b": nrm((DEPTH, W_B), 0.02),
        "ln_c_g": gain((DEPTH, W_B)),
        "ln_c_b": nrm((DEPTH, W_B), 0.02),
        "w_pb": nrm((DEPTH, W_B, D), W_B ** -0.5),
        "b_pb": nrm((DEPTH, D), 0.02),
        "w_o": nrm((DEPTH, D, D), D ** -0.5),
        "g_norm2": gain((DEPTH, D)),
        "w_ffn_in": nrm((DEPTH, D, 2 * D_FF), D ** -0.5),
        "w_ffn_out": nrm((DEPTH, D_FF, D), D_FF ** -0.5),
        "g_final": gain((D,)),
    }


def reference(x_prompt, x_sample, c_prompt, c_sample, cache_conv, w_ada, b_ada, g_norm1,
              w_in, b_in, ln_v_g, ln_v_b, w_s, b_s, w_pa, b_pa, conv_w, conv_b,
              ln_c_g, ln_c_b, w_pb, b_pb, w_o, g_norm2, w_ffn_in, w_ffn_out, g_final):
    xp, xs = x_prompt, x_sample
    conv_p, conv_s, v_p, v_s = [], [], [], []
    for l in range(DEPTH):
        params = (w_ada[l], b_ada[l], g_norm1[l], w_in[l], b_in[l], ln_v_g[l], ln_v_b[l],
                  w_s[l], b_s[l], w_pa[l], b_pa[l], conv_w[l], conv_b[l], ln_c_g[l], ln_c_b[l],
                  w_pb[l], b_pb[l], w_o[l], g_norm2[l], w_ffn_in[l], w_ffn_out[l])
        xp, buf_p, cv_p = layer(xp, c_prompt, None, *params)
        xs, buf_s, cv_s = layer(xs, c_sample, cache_conv[l], *params)
        conv_p.append(buf_p)
        conv_s.append(buf_s)
        v_p.append(cv_p)
        v_s.append(cv_s)
    y_prompt = rmsnorm(xp, g_final)
    y_sample = rmsnorm(xs, g_final)
    new_conv_prompt = jnp.stack(conv_p)
    new_conv_sample = jnp.stack(conv_s)
    chunk_v_prompt = jnp.stack(v_p)
    chunk_v_sample = jnp.stack(v_s)
    return (y_prompt, y_sample, new_conv_prompt, new_conv_sample, chunk_v_prompt, chunk_v_sample)
```

```python
import numpy as np
import concourse.bass as bass
import concourse.mybir as mybir
from concourse.bass_utils import run_bass_kernel_spmd

F32 = mybir.dt.float32
BF16 = mybir.dt.bfloat16
AF = mybir.ActivationFunctionType
ALU = mybir.AluOpType

D = 1024
KC = 8
T = 2176
NTT = 17
DFF = 2816
FC = 22
EPS = 1e-6
NCORES = 8
SB_BASE = 16512
SB_LIMIT = 229376

R_BADA, R_BIN, R_G1, R_LVG, R_LVB, R_BPA, R_CB, R_LCG, R_LCB, R_BPB, R_G2, R_CW = 0, 6, 12, 13, 14, 15, 16, 17, 18, 19, 20, 21
NVEC = 52
M_SH1, M_G1, M_GT1, M_SH2, M_G2, M_GT2 = 0, 1, 2, 3, 4, 5


class _Op:
    __slots__ = ("eng", "fn", "deps_eng", "deps_dma", "dma_key", "needs_inc", "inc_val", "sem_val")

    def __init__(self, eng, fn, deps_eng, deps_dma, dma_key):
        self.eng = eng
        self.fn = fn
        self.deps_eng = deps_eng
        self.deps_dma = deps_dma
        self.dma_key = dma_key
        self.needs_inc = False
        self.inc_val = 0
        self.sem_val = 0


class Rec:
    def __init__(self, nc):
        self.nc = nc
        self.ops = []
        self.state = {}
        self.fences = {}
        self.group_keys = set()

    def _add_dep(self, de, dd, idx):
        if idx is None:
            return
        o = self.ops[idx]
        if o.dma_key is not None:
            dd.add(idx)
        else:
            if de.get(o.eng, -1) < idx:
                de[o.eng] = idx

    def op(self, eng, fn, reads=(), writes=(), dma_key=None):
        idx = len(self.ops)
        de, dd = {}, set()
        for k in list(reads) + list(writes):
            if k not in self.state:
                f = self.fences.get(k[0])
                if f is not None:
                    for e, i in f[0].items():
                        if de.get(e, -1) < i:
                            de[e] = i
                    dd |= f[1]
        for k in reads:
            st = self.state.get(k)
            if st is not None:
                self._add_dep(de, dd, st[0])
        for k in writes:
            st = self.state.get(k)
            if st is not None:
                self._add_dep(de, dd, st[0])
                for e, i in st[1].items():
                    if de.get(e, -1) < i:
                        de[e] = i
                dd |= st[2]
        self.ops.append(_Op(eng, fn, de, dd, dma_key))
        is_dma = dma_key is not None
        for k in reads:
            st = self.state.setdefault(k, [None, {}, set()])
            if is_dma:
                st[2].add(idx)
            else:
                st[1][eng] = idx
        for k in writes:
            self.state[k] = [idx, {}, set()]
        return idx

    def fence_into(self, new_region, olds):
        de, dd = {}, set()
        for r in olds:
            self.fence(r)
            f = self.fences[r]
            for e, i in f[0].items():
                if de.get(e, -1) < i:
                    de[e] = i
            dd |= f[1]
        old = self.fences.get(new_region)
        if old is not None:
            for e, i in old[0].items():
                if de.get(e, -1) < i:
                    de[e] = i
            dd |= old[1]
        self.fence(new_region)
        f = self.fences[new_region]
        for e, i in f[0].items():
            if de.get(e, -1) < i:
                de[e] = i
        dd |= f[1]
        self.fences[new_region] = (de, dd)

    def fence(self, region):
        de, dd = {}, set()
        old = self.fences.get(region)
        if old is not None:
            de.update(old[0])
            dd |= old[1]
        for k in [k for k in self.state if k[0] == region]:
            st = self.state.pop(k)
            self._add_dep(de, dd, st[0])
            for e, i in st[1].items():
                if de.get(e, -1) < i:
                    de[e] = i
            dd |= st[2]
        self.fences[region] = (de, dd)

    def emit(self):
        nc = self.nc
        engs = {"pe": nc.tensor, "act": nc.scalar, "dve": nc.vector, "pool": nc.gpsimd, "sp": nc.sync}
        ops = self.ops
        for o in ops:
            for e, i in o.deps_eng.items():
                if e == "pe" and o.eng == "pe" and o.dma_key is None:
                    continue
                ops[i].needs_inc = True
        cnt = {e: 0 for e in engs}
        dma_cnt = {}
        for o in ops:
            if o.dma_key is not None:
                dma_cnt[o.dma_key] = dma_cnt.get(o.dma_key, 0) + 16
                o.sem_val = dma_cnt[o.dma_key]
            elif o.needs_inc:
                cnt[o.eng] += 1
                o.inc_val = cnt[o.eng]
        for o in ops:
            if o.dma_key in self.group_keys:
                o.sem_val = dma_cnt[o.dma_key]
        esem = {e: nc.alloc_semaphore("s_" + e) for e in engs}
        dsem = {}
        for k in dma_cnt:
            dsem[k] = nc.alloc_semaphore("d_" + "_".join(str(x) for x in k))
        waited = {}
        for o in ops:
            e = engs[o.eng]
            waits = {}
            for de, i in o.deps_eng.items():
                if de == "pe" and o.eng == "pe" and o.dma_key is None:
                    continue
                s = esem[de]
                v = ops[i].inc_val
                if waits.get(s, (0, 0))[1] < v:
                    waits[s] = (s, v)
            for i in o.deps_dma:
                s = dsem[ops[i].dma_key]
                v = ops[i].sem_val
                if waits.get(s, (0, 0))[1] < v:
                    waits[s] = (s, v)
            for s, v in waits.values():
                kk = (o.eng, s.num if hasattr(s, "num") else id(s))
                if waited.get(kk, 0) < v:
                    e.wait_ge(s, v)
                    waited[kk] = v
            ins = o.fn(e)
            if o.dma_key is not None:
                ins.then_inc(dsem[o.dma_key], 16)
            elif o.needs_inc:
                ins.then_inc(esem[o.eng], 1)
        return len(ops), len(dsem) + len(esem)


class Mem:
    def __init__(self, nc):
        self.nc = nc
        self.n = 0

    def at(self, name, shape, dtype, off):
        assert off % 32 == 0, (name, off)
        nbytes = int(np.prod(shape[1:])) * (2 if dtype == BF16 else 4)
        assert SB_BASE + off + nbytes <= SB_LIMIT, (name, off, nbytes)
        self.n += 1
        return self.nc.alloc_sbuf_tensor_at(f"{name}_{self.n}", list(shape), dtype, offset=SB_BASE + off).ap()


def build(dbg=()):
    nc = bass.Bass("TRN2", target_bir_lowering=False)
    R = Rec(nc)
    M = Mem(nc)
    dbg = set(dbg)
    dbg_outs = {}

    def din(name, shape):
        return nc.dram_tensor(name, list(shape), F32, kind="ExternalInput").ap()

    def dout(name, shape):
        return nc.dram_tensor(name, list(shape), F32, kind="ExternalOutput").ap()

    xc = din("xc", [T, D])
    cc = din("cc", [17, D])
    cache = din("cache", [16, 30, D])
    w_ada = din("w_ada", [D, 6 * D])
    b_ada = din("b_ada", [6, D])
    g_norm1 = din("g_norm1", [1, D])
    w_in = din("w_in", [D, 6 * D])
    b_in = din("b_in", [6, D])
    ln_v_g = din("ln_v_g", [1, D])
    ln_v_b = din("ln_v_b", [1, D])
    w_s = din("w_s", [4, 128, 128])
    b_s = din("b_s", [1, 512])
    w_pa = din("w_pa", [D, D])
    b_pa = din("b_pa", [1, D])
    conv_w = din("conv_w", [31, D])
    conv_b = din("conv_b", [1, D])
    ln_c_g = din("ln_c_g", [1, D])
    ln_c_b = din("ln_c_b", [1, D])
    w_pb = din("w_pb", [D, D])
    b_pb = din("b_pb", [1, D])
    w_o = din("w_o", [D, D])
    g_norm2 = din("g_norm2", [1, D])
    w_ffn_in = din("w_ffn_in", [D, 2 * DFF])
    w_ffn_out = din("w_ffn_out", [DFF, D])
    g_final = din("g_final", [1, D])

    y_out = dout("y", [T, D])
    ncp_out = dout("ncp", [30, D])
    ncs_out = dout("ncs", [16, 30, D])
    cvp_out = dout("cvp", [128, D])
    cvs_out = dout("cvs", [128, D])

    o = 0
    ident32 = M.at("ident32", [128, 128], F32, o); o += 512
    onesb = M.at("onesb", [128, 128], BF16, o); o += 256
    onesm = M.at("onesm", [128, 128], BF16, o); o += 256
    vecT = M.at("vecT", [128, KC, NVEC], F32, o); o += KC * NVEC * 4
    modT = M.at("modT", [128, 48, 17], F32, o); o += 48 * 17 * 4
    cT = M.at("cT", [128, KC, 17], BF16, o); o += 288
    WT = M.at("WT", [128, 4, 128], BF16, o); o += 1024
    WTS = M.at("WTS", [128, 4, 128], BF16, o); o += 1024
    ssum = M.at("ssum", [128, 64], F32, o); o += 256
    rstd = M.at("rstd", [128, 64], F32, o); o += 256
    st6 = M.at("st6", [128, 2, 2, 6], F32, o); o += 96
    mv = M.at("mv", [128, 2, 2], F32, o); o += 32
    epsT = M.at("epsT", [128, 1], F32, o); o += 32
    identb = M.at("identb", [128, 128], BF16, o); o += 256
    negh = M.at("negh", [128, 1], F32, o); o += 32
    o = (o + 255) // 256 * 256
    CONST_END = o
    WOFF = [o, o + 16384]
    o += 32768
    OFF_A = o; o += 34816
    OFF_C = o; o += 34816
    OFF_B = o; o += 34816
    OFF_D = o; o += 43008
    OFF_S = o
    SCR_SIZE = (SB_LIMIT - SB_BASE) - OFF_S
    assert SCR_SIZE >= 22528, SCR_SIZE

    Wb = [M.at(f"W{i}", [128, 8192], BF16, WOFF[i]) for i in range(2)]

    def wview(slot, kch, ncols):
        return Wb[slot][:, 0:kch * ncols].rearrange("p (k n) -> p k n", k=kch)

    PS = [nc.alloc_psum_tensor(f"ps{i}", [128, 1024], F32).ap() for i in range(4)]

    def bank(i):
        return PS[i // 2][:, (i % 2) * 512:(i % 2) * 512 + 512]

    hT = M.at("hT", [128, KC, T], BF16, OFF_A)
    nB = M.at("nB", [128, NTT, D], BF16, OFF_B)
    U = M.at("U", [128, KC, T], BF16, OFF_C)
    SA = M.at("SA", [128, KC, T], BF16, OFF_B)

    ST512 = [(0, 512), (512, 512), (1024, 512), (1536, 512), (2048, 128)]
    STMIX = [(0, 512), (512, 512), (1024, 384), (1408, 384), (1792, 384)]

    def tts(t0, w):
        return range(t0 // 128, (t0 + w + 127) // 128)

    def dma(eng, out, in_, reads, writes, key):
        R.op(eng, lambda e: e.dma_start(out=out, in_=in_), reads=reads, writes=writes, dma_key=key)

    def mm(out, lhsT, rhs, start, stop, reads, writes):
        R.op("pe", lambda e: e.matmul(out, lhsT=lhsT, rhs=rhs, start=start, stop=stop), reads=reads, writes=writes)

    def tp(out, in_, nrows, reads, writes):
        R.op("pe", lambda e: e.transpose(out, in_, ident32[0:nrows, 0:nrows]), reads=list(reads) + [("ident",)], writes=writes)

    def act(out, in_, func, reads, writes, bias=None, scale=None, accum_out=None):
        kw = {}
        if bias is not None:
            kw["bias"] = bias
        if scale is not None:
            kw["scale"] = scale
        if accum_out is not None:
            kw["accum_out"] = accum_out
        R.op("act", lambda e: e.activation(out=out, in_=in_, func=func, **kw), reads=reads, writes=writes)

    def tt_(eng, out, in0, in1, op, reads, writes):
        R.op(eng, lambda e: e.tensor_tensor(out=out, in0=in0, in1=in1, op=op), reads=reads, writes=writes)

    def ts_(eng, out, in0, s1, s2, op0, op1, reads, writes):
        if s2 is None:
            R.op(eng, lambda e: e.tensor_scalar(out=out, in0=in0, scalar1=s1, scalar2=None, op0=op0), reads=reads, writes=writes)
        else:
            R.op(eng, lambda e: e.tensor_scalar(out=out, in0=in0, scalar1=s1, scalar2=s2, op0=op0, op1=op1), reads=reads, writes=writes)

    def stt(out, in0, scalar, in1, op0, op1, reads, writes):
        R.op("dve", lambda e: e.scalar_tensor_tensor(out=out, in0=in0, scalar=scalar, in1=in1, op0=op0, op1=op1), reads=reads, writes=writes)

    def cp(eng, out, in_, reads, writes):
        R.op(eng, lambda e: e.tensor_copy(out=out, in_=in_), reads=reads, writes=writes)

    def dump(name, ap, shape, reads):
        if name not in dbg:
            return
        d = dout("dbg_" + name, shape)
        dbg_outs[name] = shape
        dma("sp" if ap.dtype == F32 else "pool", d, ap, reads, [("dbgout", name)], ("dbg", name))

    wstate = {"n": 0}

    def wload(src3, kch, ncols, extra_reads=()):
        slot = wstate["n"] % 2
        wstate["n"] += 1
        v = wview(slot, kch, ncols)
        dma("pool", v, src3, list(extra_reads), [("W", slot, 0), ("W", slot, 1)], ("W", slot, 0))
        return slot, v

    psrot = {"n": 0}

    def next_bank():
        b = psrot["n"] % 4
        psrot["n"] += 1
        return b

    def next_pair():
        psrot["n"] = (psrot["n"] + 1) // 2 * 2
        p = (psrot["n"] // 2) % 2
        psrot["n"] += 2
        return p

    R.op("pool", lambda e: e.memset(ident32, 1.0), writes=[("ident",)])
    R.op("pool", lambda e: e.affine_select(out=ident32, in_=ident32, pattern=[[-1, 128]], compare_op=ALU.is_equal,
                                           fill=0.0, base=0, channel_multiplier=1), reads=[("ident",)], writes=[("ident",)])
    R.op("dve", lambda e: e.memset(onesb, 1.0), writes=[("onesb",)])
    R.op("dve", lambda e: e.memset(onesm, 1.0 / 1024.0), writes=[("onesm",)])
    R.op("dve", lambda e: e.memset(epsT, EPS), writes=[("epsT",)])
    cp("dve", identb, ident32, [("ident",)], [("identb",)])
    R.op("dve", lambda e: e.memset(negh, -0.5), writes=[("negh",)])

    wa3 = w_ada.rearrange("(k p) n -> p k n", p=128)
    ADA = [M.at(f"ADA{i}", [128, KC, 1024], BF16, OFF_D + i * 16384) for i in range(2)]

    def ada_load(blk, extra_reads=(), buf=None):
        bi = blk % 2 if buf is None else buf
        dma("pool", ADA[bi], wa3[:, :, blk * 1024:(blk + 1) * 1024], list(extra_reads), [("D", "ada", bi)], ("ada", bi))

    def ada_compute(blk, buf=None, rot=False):
        bi = blk % 2 if buf is None else buf
        wv_ = ADA[bi]
        bk = next_bank() if rot else 4 + (blk % 2)
        pm = bank(bk)[:, 0:KC * 17].rearrange("p (j s) -> p j s", j=KC)
        for j in range(KC):
            for k in range(KC):
                mm(pm[:, j, :], wv_[:, k, j * 128:(j + 1) * 128], cT[:, k, :], k == 0, k == KC - 1,
                   [("D", "ada", bi), ("cT",)], [("ps", bk)])
        tt_("dve", modT[:, blk * 8:(blk + 1) * 8, :], pm, vecT[:, :, R_BADA + blk:R_BADA + blk + 1].to_broadcast([128, KC, 17]),
            ALU.add, [("ps", bk), ("vecT",)], [("modT", blk)])
        if blk in (M_G1, M_G2):
            gr = R_G1 if blk == M_G1 else R_G2
            ts_("dve", modT[:, blk * 8:(blk + 1) * 8, :], modT[:, blk * 8:(blk + 1) * 8, :], 1.0, None, ALU.add, None,
                [("modT", blk)], [("modT", blk)])
            tt_("dve", modT[:, blk * 8:(blk + 1) * 8, :], modT[:, blk * 8:(blk + 1) * 8, :],
                vecT[:, :, gr:gr + 1].to_broadcast([128, KC, 17]), ALU.mult, [("modT", blk), ("vecT",)], [("modT", blk)])

    BV2 = M.at("BV2", [2, D], BF16, OFF_D + 40960)
    bvH = BV2[0:1, :]
    dma("pool", bvH, b_in[1:2, :], [], [("E", "bvH")], ("m", 0))
    ada_load(0)
    ada_load(1)
    wi3 = w_in.rearrange("(k p) n -> p k n", p=128)
    slot_v, wv_v = wload(wi3[:, :, 1024:2048], KC, 1024, extra_reads=[("D", "ada", 1)])
    VEC = M.at("VEC", [NVEC, D], F32, OFF_S)
    CC = M.at("CC", [17, D], F32, OFF_S + 4096)
    WS32 = M.at("WS32", [128, 4, 128], F32, OFF_B)
    WTf = M.at("WTf", [128, 4, 128], F32, OFF_S + 14336)
    A8 = M.at("A8", [8, 4, 8], F32, OFF_B + 2048)
    Rm = M.at("Rm", [8, 128], F32, OFF_S + 16384 + 128)
    O1 = M.at("O1", [8, 4, 128], F32, OFF_S + 16384 + 640)
    R.group_keys.add(("setup",))
    R.group_keys.add(("setup2",))
    vec_srcs = [(R_BADA, 6, b_ada), (R_BIN, 6, b_in), (R_G1, 1, g_norm1), (R_LVG, 1, ln_v_g), (R_LVB, 1, ln_v_b),
                (R_BPA, 1, b_pa), (R_CB, 1, conv_b), (R_LCG, 1, ln_c_g), (R_LCB, 1, ln_c_b), (R_BPB, 1, b_pb),
                (R_G2, 1, g_norm2), (R_CW, 31, conv_w)]
    dma("sp", CC, cc, [], [("S", "CC")], ("setup",))
    for r0, nr, src in vec_srcs:
        dma("sp", VEC[r0:r0 + nr, :], src, [], [("S", "VEC", r0)], ("setup",))
    vec_keys = [("S", "VEC", r0) for r0, _, _ in vec_srcs]
    dma("sp", WS32, w_s.rearrange("g t s -> t g s"), [], [("B", 0)], ("setup2",))
    dma("sp", A8, w_s[:, 0:8, 0:8].rearrange("g b a -> b g a"), [], [("B", 1)], ("setup2",))

    pv = PS[3][:, 0:KC * NVEC].rearrange("p (c r) -> p c r", c=KC)
    for c in range(KC):
        tp(pv[:, c, :], VEC[:, c * 128:(c + 1) * 128], NVEC, vec_keys, [("ps", 6)])
    cp("dve", vecT, pv, [("ps", 6)], [("vecT",)])

    act(CC, CC, AF.Silu, [("S", "CC")], [("S", "CC")])
    pc = PS[3][:, 512:512 + KC * 17].rearrange("p (c r) -> p c r", c=KC)
    for c in range(KC):
        tp(pc[:, c, :], CC[:, c * 128:(c + 1) * 128], 17, [("S", "CC")], [("ps", 7)])
    cp("dve", cT, pc, [("ps", 7)], [("cT",)])
    ada_compute(0)
    ada_compute(1)

    dump("vecT", vecT, [128, KC, NVEC], [("vecT",)])
    dump("WT", WT, [128, 4, 128], [("WT",)], ) if False else None

    def modv(mi, tt):
        if tt < 16:
            return modT[:, mi * 8:(mi + 1) * 8, 16:17].to_broadcast([128, KC, 128])
        return modT[:, mi * 8:(mi + 1) * 8, 0:16].unsqueeze(3).to_broadcast([128, KC, 16, 8])

    def tokv(ap3, tt):
        if tt < 16:
            return ap3
        return ap3.rearrange("p k (q b) -> p k q b", b=8)

    def interleave(ga, na, gb, nb):
        ia = ib = 0
        ea = eb = False
        while not (ea and eb):
            if not ea and (eb or ia * nb <= ib * na):
                try:
                    next(ga)
                    ia += 1
                except StopIteration:
                    ea = True
            else:
                try:
                    next(gb)
                    ib += 1
                except StopIteration:
                    eb = True

    def drain(g):
        for _ in g:
            pass

    ncnt = {"n": 0}

    def newcol():
        ncnt["n"] += 1
        return ncnt["n"] % 64

    def rstd_from(col, src_ap, src_keys, scale):
        act(rstd[:, col:col + 1], src_ap, AF.Sqrt, list(src_keys) + [("epsT",)], [("rstd", col)], bias=epsT[:, 0:1], scale=scale)
        R.op("dve", lambda e: e.reciprocal(out=rstd[:, col:col + 1], in_=rstd[:, col:col + 1]), reads=[("rstd", col)], writes=[("rstd", col)])

    def rmsnorm_stages(src, src_keys, xn, kxn, sqj, pr, tt, dst_tile, dst_keys, mg, msh):
        col = newcol()
        pt = PS[pr].rearrange("p (c t) -> p c t", c=KC)
        pk = [("ps", 2 * pr), ("ps", 2 * pr + 1)]

        def sa():
            act(sqj, src, AF.Square, src_keys, [("S", "sqj"), ("ssum", col)], accum_out=ssum[:, col:col + 1])
            rstd_from(col, ssum[:, col:col + 1], [("ssum", col)], 1.0 / D)

        def sb():
            act(xn, src, AF.Identity, list(src_keys) + [("rstd", col)], [kxn], scale=rstd[:, col:col + 1])
            for c in range(KC):
                tp(pt[:, c, :], xn[:, c * 128:(c + 1) * 128], 128, [kxn], [("ps", 2 * pr + c // 4)])

        def sc():
            tt_("dve", tokv(pt, tt), tokv(pt, tt), modv(mg, tt), ALU.mult, pk + [("modT", mg)], pk)
            tt_("dve", tokv(dst_tile, tt), tokv(pt, tt), modv(msh, tt), ALU.add, pk + [("modT", msh)], dst_keys)
        return sa, sb, sc

    def skewed(stage_lists, pre=None, reverse=False, post=None):
        n = len(stage_lists[0])
        nt = len(stage_lists)
        for step in range(nt + n - 1):
            for si in (range(n - 1, -1, -1) if reverse else range(n)):
                t = step - si
                if 0 <= t < nt:
                    if si == 0 and pre is not None:
                        pre(t)
                    stage_lists[t][si]()
                    if post is not None and si == post[0]:
                        post[1](t)
            yield

    OUTK = []

    def hkey(k, tt):
        return ("A", tt)

    brot = {"n": 0}

    def proj_fm_gen(slot, wv, kch, in_buf, in_key, njc, evac, sts=STMIX, banks=None):
        for j in range(njc):
            for (t0, w) in sts:
                if banks is None:
                    b = next_bank()
                else:
                    b = banks[brot["n"] % len(banks)]
                    brot["n"] += 1
                pb = bank(b)[:, 0:w]
                for k in range(kch):
                    rk = [("W", slot, 0), ("W", slot, 1)] + [in_key(k, tt) for tt in tts(t0, w)]
                    mm(pb, wv[:, k, j * 128:(j + 1) * 128], in_buf[:, k, t0:t0 + w], k == 0, k == kch - 1, rk, [("ps", b)])
                evac(j, t0, w, pb, ("ps", b))
                yield

    def proj_fm(w3, kch, in_buf, in_key, njc, evac, sts=STMIX):
        slot, wv = wload(w3, kch, njc * 128)
        drain(proj_fm_gen(slot, wv, kch, in_buf, in_key, njc, evac, sts))

    LVGs = M.at("LVGs", [128, D], F32, OFF_D + 32768)
    LVBs = M.at("LVBs", [128, D], F32, OFF_D + 36864)
    bvL = M.at("bvL", [1, D], BF16, OFF_B + 32768)
    bv32 = M.at("bv32", [1, D], F32, OFF_B + 28672)
    dma("sp", bv32, b_in[1:2, :], [], [("E", "bv32")], ("m", 1))
    dma("sp", LVGs, ln_v_g.partition_broadcast(128), [], [("E", "LVG")], ("m", 2))
    dma("sp", LVBs, ln_v_b.partition_broadcast(128), [], [("E", "LVB")], ("m", 3))
    tt_("dve", bvL, bv32, bvH, ALU.subtract, [("E", "bv32"), ("E", "bvH")], [("E", "bvL")])
    dma("sp", BV2[1:2, :], bvL, [("E", "bvL")], [("E", "bvL2")], ("m", 4))

    R.fence("S")
    NXIN = 8
    XIN = [M.at(f"xin{i}", [128, D], F32, OFF_C + i * 4096) for i in range(NXIN)]
    XN = [M.at(f"xn{i}", [128, D], F32, OFF_S + 12288 + i * 4096) for i in range(2)]
    SQJ = M.at("sqj", [128, D], BF16, OFF_S + 20480)
    n1 = []
    for tt in range(NTT):
        s = tt % NXIN
        n1.append(rmsnorm_stages(XIN[s], [("C", "xin", s)], XN[tt % 2], ("S", "xn", tt % 2), SQJ, 2 + tt % 2, tt,
                                 hT[:, :, tt * 128:(tt + 1) * 128], [("A", tt)], M_G1, M_SH1))

    def pre_x(tt):
        s = tt % NXIN
        dma("sp", XIN[s], xc[tt * 128:(tt + 1) * 128, :], [("D", "ada", 1)] if 3 <= tt < NXIN else [], [("C", "xin", s)], ("xin", s))
    for tt in range(NXIN):
        pre_x(tt)

    def pre_x2(tt):
        if tt + NXIN < NTT and tt >= 0:
            pre_x(tt + NXIN)
    drain(skewed(n1, post=(1, pre_x2)))
    R.fence("C")
    slot_u, wv_u = wload(wi3[:, :, 0:1024], KC, 1024, extra_reads=[("A", 2)])
    ada_load(2, extra_reads=[("A", 13)])
    ada_load(3, extra_reads=[("A", 13)])
    dump("hT", hT, [128, KC, T], [("A", tt) for tt in range(NTT)])

    R.fence("S")
    VR = [M.at(f"vr{i}", [128, D], F32, OFF_S + i * 4096) for i in range(3)]
    LVG = PS[2]
    LVB = PS[3]
    cp("dve", LVG, LVGs, [("E", "LVG")], [("ps", 4), ("ps", 5)])
    cp("dve", LVB, LVBs, [("E", "LVB")], [("ps", 6), ("ps", 7)])

    def v_stages(tt):
        slot, wv = slot_v, wv_v
        vr = VR[tt % 3]
        kv = ("S", "vr", tt % 3)
        col = newcol()

        def sa():
            p = next_pair()
            pp = PS[p]
            pk = [("ps", 2 * p), ("ps", 2 * p + 1)]
            for h in range(2):
                o_ = pp[:, h * 512:(h + 1) * 512]
                for k in range(KC):
                    mm(o_, hT[:, k, tt * 128:(tt + 1) * 128], wv[:, k, h * 512:(h + 1) * 512], k == 0, False,
                       [("A", tt), ("W", slot, 0), ("W", slot, 1)], [pk[h]])
                mm(o_, onesb[0:2, :], BV2[0:2, h * 512:(h + 1) * 512], False, True, [("onesb",), ("E", "bvH"), ("E", "bvL2")], [pk[h]])
            for h in range(2):
                act(vr[:, h * 512:(h + 1) * 512], pp[:, h * 512:(h + 1) * 512], AF.Gelu_apprx_tanh, [pk[h]],
                    [kv])

        def sb():
            for h in range(2):
                R.op("dve", (lambda o2, i2: (lambda e: e.bn_stats(out=o2, in_=i2)))(st6[:, tt % 2, h, :], vr[:, h * 512:(h + 1) * 512]),
                     reads=[kv], writes=[("st6", tt % 2, h)])
            R.op("dve", (lambda o2, i2: (lambda e: e.bn_aggr(out=o2, in_=i2)))(mv[:, tt % 2, :], st6[:, tt % 2, :, :].rearrange("p h s -> p (h s)")),
                 reads=[("st6", tt % 2, 0), ("st6", tt % 2, 1)], writes=[("mv", tt % 2)])
            act(rstd[:, col:col + 1], mv[:, tt % 2, 1:2], AF.Sqrt, [("mv", tt % 2), ("epsT",)], [("rstd", col)], bias=epsT[:, 0:1], scale=1.0)

        def sc():
            R.op("dve", lambda e: e.reciprocal(out=rstd[:, col:col + 1], in_=rstd[:, col:col + 1]), reads=[("rstd", col)], writes=[("rstd", col)])
            ts_("dve", vr, vr, mv[:, tt % 2, 0:1], rstd[:, col:col + 1], ALU.subtract, ALU.mult, [kv, ("mv", tt % 2), ("rstd", col)], [kv])
            tt_("dve", vr, vr, LVG, ALU.mult, [kv, ("ps", 4), ("ps", 5)], [kv])
            if tt < 15:
                tt_("dve", nB[:, tt, :], vr, LVB, ALU.add, [kv, ("ps", 6), ("ps", 7)], [("B", tt)])
            else:
                tt_("dve", vr, vr, LVB, ALU.add, [kv, ("ps", 6), ("ps", 7)], [kv])
                cp("dve", nB[:, tt, :], vr, [kv], [("B", tt)])
                dma("sp", cvp_out if tt == 15 else cvs_out, vr, [kv], [("out", "cv", tt)], ("cvout", tt))
                OUTK.append(("out", "cv", tt))
        return sa, sb, sc

    def gen_v():
        return skewed([v_stages(tt) for tt in range(NTT)])

    def evac_u(j, t0, w, pb, bk):
        act(U[:, j, t0:t0 + w], pb, AF.Gelu_apprx_tanh, [bk, ("vecT",)], [("C", j, tt) for tt in tts(t0, w)],
            bias=vecT[:, j, R_BIN + 0:R_BIN + 1])
    bs32 = M.at("bs32", [1, 2, 512], F32, OFF_D + 32768)
    BS2 = M.at("BS2", [2, 2, 512], BF16, OFF_D + 36864)
    bsH = BS2[0:1, :, :]
    bsL = M.at("bsL", [1, 2, 512], BF16, OFF_D + 38912)

    def wt_stage1():
        R.op("pool", lambda e: e.memset(Rm, 1.0), writes=[("S", "Rm")])
        R.op("pool", lambda e: e.affine_select(out=Rm, in_=Rm, pattern=[[0, 16], [1, 8]], compare_op=ALU.is_equal, fill=0.0,
                                               base=0, channel_multiplier=-1), reads=[("S", "Rm")], writes=[("S", "Rm")])
        b1 = next_bank()
        pw = bank(b1).rearrange("p (g t) -> p g t", g=4)
        for g in range(4):
            tp(pw[:, g, :], WS32[:, g, :], 128, [("B", 0)], [("ps", b1)])
        cp("dve", WTf, pw, [("ps", b1)], [("S", "WTf")])
        R.op("pool", lambda e: e.affine_select(out=WT, in_=WTf, pattern=[[0, 4], [1, 128]], compare_op=ALU.is_ge, fill=0.0,
                                               base=0, channel_multiplier=-1), reads=[("S", "WTf")], writes=[("WT",)])
        b2 = next_bank()
        po1 = bank(b2)[0:8, :].rearrange("p (g t) -> p g t", g=4)
        for g in range(4):
            mm(po1[:, g, :], A8[:, g, :], Rm, True, True, [("B", 1), ("S", "Rm")], [("ps", b2)])
        cp("dve", O1, po1, [("ps", b2)], [("S", "O1")])
        dma("sp", bs32[:, 0, :], b_s, [], [("S", "bs32"), ("E", "LVG")], ("m", 1))

    def wt_stage2():
        b1 = next_bank()
        pw = bank(b1).rearrange("p (g t) -> p g t", g=4)
        for g in range(4):
            mm(pw[:, g, :], Rm, O1[:, g, :], True, True, [("S", "O1"), ("S", "Rm")], [("ps", b1)])
        cp("dve", WTf, pw, [("ps", b1)], [("S", "WTf")])
        R.op("pool", lambda e: e.affine_select(out=WTf, in_=WTf, pattern=[[0, 4], [8, 16], [1, 8]], compare_op=ALU.is_ge, fill=0.0,
                                               base=0, channel_multiplier=-1), reads=[("S", "WTf")], writes=[("S", "WTf")])
        R.op("pool", lambda e: e.affine_select(out=WTS, in_=WTf, pattern=[[0, 4], [-8, 16], [0, 8]], compare_op=ALU.is_ge, fill=0.0,
                                               base=0, channel_multiplier=1), reads=[("S", "WTf")], writes=[("WTS",)])
        cp("dve", bs32[:, 1, :].rearrange("p (g q b) -> p g q b", g=4, q=16),
           bs32[:, 0, :].rearrange("p (g t) -> p g t", g=4)[:, :, 0:8].unsqueeze(2).to_broadcast([1, 4, 16, 8]),
           [("S", "bs32")], [("S", "bs32b")])
        cp("dve", bsH, bs32, [("S", "bs32"), ("S", "bs32b")], [("S", "bsH"), ("E", "LVB")])
        tt_("dve", bsL, bs32, bsH, ALU.subtract, [("S", "bs32"), ("S", "bs32b"), ("S", "bsH")], [("S", "bsL"), ("E", "LVB")])
        dma("sp", BS2[1:2, :, :], bsL, [("S", "bsL")], [("S", "bsL2")], ("m", 4))

    wt_stage1()
    vgen = gen_v()
    ugen = proj_fm_gen(slot_u, wv_u, KC, hT, hkey, 8, evac_u)
    next(vgen)
    next(ugen)
    next(vgen)
    next(ugen)
    wt_stage2()
    for _ in range(4):
        next(vgen)
        next(ugen)
    ada_compute(2, rot=True)
    ada_compute(3, rot=True)
    wv_ga = ADA[0]
    dma("pool", wv_ga, wi3[:, :, 4096:5120], [], [("D", "ada", 0)], ("ada", 0))
    ada_load(4, buf=1)
    interleave(vgen, NTT - 4, ugen, 20)
    dump("n", nB, [128, NTT, D], [("B", tt) for tt in range(NTT)])
    dump("U", U, [128, KC, T], [("C", j, tt) for j in range(KC) for tt in range(NTT)])

    R.fence("S")

    def gen_mix():
        for tt in range(NTT):
            gi = 0 if tt < 16 else 1
            Wm, wk = (WT, ("WT",)) if tt < 16 else (WTS, ("WTS",))
            for jg in range(2):
                b = next_bank()
                pb = bank(b).rearrange("p (j t) -> p j t", j=4)
                for jj in range(4):
                    j = jg * 4 + jj
                    g = j // 2
                    mm(pb[:, jj, :], nB[:, tt, j * 128:(j + 1) * 128], Wm[:, g, :], True, False, [("B", tt), wk], [("ps", b)])
                    mm(pb[:, jj, :], onesb[0:2, :], BS2[0:2, gi, g * 128:(g + 1) * 128], False, True, [("onesb",), ("S", "bsH"), ("S", "bsL2")], [("ps", b)])
                uk = [("C", j, tt) for j in range(jg * 4, jg * 4 + 4)]
                uv = U[:, jg * 4:(jg + 1) * 4, tt * 128:(tt + 1) * 128]
                tt_("dve", uv, pb, uv, ALU.mult, [("ps", b)] + uk, uk)
            yield
    ada_compute(4, buf=1)
    ada_load(5, buf=1)
    ga_jobs = [(j, t0, w) for j in range(KC) for (t0, w) in STMIX]

    def ga_alias(j, t0, w):
        return (j * 4352 + 2 * t0) // 2048, (j * 4352 + 2 * (t0 + w) - 1) // 2048

    def emit_ga(j, t0, w):
        b = next_bank()
        pb = bank(b)[:, 0:w]
        for k in range(KC):
            rk = [("D", "ada", 0)] + [("A", tt) for tt in tts(t0, w)]
            mm(pb, wv_ga[:, k, j * 128:(j + 1) * 128], hT[:, k, t0:t0 + w], k == 0, k == KC - 1, rk, [("ps", b)])
        lo, hi = ga_alias(j, t0, w)
        act(SA[:, j, t0:t0 + w], pb, AF.Sigmoid, [("ps", b), ("vecT",)],
            [("B", j, tt) for tt in tts(t0, w)] + [("B", t2) for t2 in range(lo, hi + 1)], bias=vecT[:, j, R_BIN + 4:R_BIN + 5])

    mixg = gen_mix()
    gi_ = 0
    for t in range(NTT):
        next(mixg)
        for _ in range(3):
            if gi_ < len(ga_jobs) and ga_alias(*ga_jobs[gi_])[1] <= t:
                emit_ga(*ga_jobs[gi_])
                gi_ += 1
    while gi_ < len(ga_jobs):
        emit_ga(*ga_jobs[gi_])
        gi_ += 1
    dump("ug", U, [128, KC, T], [("C", j, tt) for j in range(KC) for tt in range(NTT)])

    def evac_sig(dst, reg, brow):
        def f(j, t0, w, pb, bk):
            act(dst[:, j, t0:t0 + w], pb, AF.Sigmoid, [bk, ("vecT",)], [(reg, j, tt) for tt in tts(t0, w)],
                bias=vecT[:, j, brow:brow + 1])
        return f
    ada_compute(5, buf=1)
    dump("modT", modT, [128, 48, 17], [("modT", i) for i in range(6)])

    def evac_ma(j, t0, w, pb, bk):
        ks = [("B", j, tt) for tt in tts(t0, w)]
        stt(SA[:, j, t0:t0 + w], pb, vecT[:, j, R_BPA:R_BPA + 1], SA[:, j, t0:t0 + w], ALU.add, ALU.mult, [bk, ("vecT",)] + ks, ks)
    proj_fm(w_pa.rearrange("(k p) n -> p k n", p=128), KC, U, lambda k, tt: ("C", k, tt), 8, evac_ma)
    dump("MA", SA, [128, KC, T], [("B", j, tt) for j in range(KC) for tt in range(NTT)])

    R.fence("C")
    R.fence("S")
    R.fence("D")
    R.fence("E")
    SGB = U
    proj_fm(wi3[:, :, 3072:4096], KC, hT, hkey, 8, evac_sig(SGB, "C", R_BIN + 3))
    GP = M.at("GP", [128, KC, 2080], BF16, OFF_D)
    GS = M.at("GS", [128, KC, 16, 38], BF16, OFF_D + 33280)
    GF32 = M.at("GF32", [128, KC, 160], F32, OFF_S)
    CA = [M.at(f"CA{i}", [128, D], F32, OFF_S + 5120 + i * 4096) for i in range(2)]
    STG = [M.at(f"STG{i}", [128, D], F32, OFF_S + 13312 + i * 4096) for i in range(2)]
    R.op("dve", lambda e: e.memset(GP[:, :, 0:30], 0.0), writes=[("D", "hist")])
    for gq in range(4):
        s_ = gq % 2
        dma("sp", CA[s_][0:120, :], cache[gq * 4:(gq + 1) * 4].rearrange("q r d -> (q r) d"), [], [("S", "ca", s_)], ("ca", s_))
        pr = 2 + (gq % 2)
        pt = PS[pr].rearrange("p (c t) -> p c t", c=KC)
        for c in range(KC):
            tp(pt[:, c, 0:120], CA[s_][0:120, c * 128:(c + 1) * 128], 120, [("S", "ca", s_)], [("ps", 2 * pr + c // 4)])
        cp("dve", GS[:, :, gq * 4:(gq + 1) * 4, 0:30], pt[:, :, 0:120].rearrange("p c (q r) -> p c q r", q=4),
           [("ps", 2 * pr), ("ps", 2 * pr + 1)], [("D", "gsh", gq)])

    def evac_glu(j, t0, w, pb, bk):
        bga = vecT[:, j, R_BIN + 2:R_BIN + 3]
        ck = [("C", j, tt) for tt in tts(t0, w)]
        if t0 < 2048:
            stt(GP[:, j, 30 + t0:30 + t0 + w], pb, bga, SGB[:, j, t0:t0 + w], ALU.add, ALU.mult, [bk, ("vecT",)] + ck,
                [("D", j, tt) for tt in tts(t0, w)])
            if t0 == 1536:
                stt(GF32[:, j, 0:32], pb[:, 480:512], bga, SGB[:, j, 2016:2048], ALU.add, ALU.mult, [bk, ("vecT",)] + ck, [("S", "gf", j, 0)])
        else:
            stt(GS[:, j, :, 30:38], pb.rearrange("p (q b) -> p q b", b=8), bga, SGB[:, j, 2048:2176].rearrange("p (q b) -> p q b", b=8),
                ALU.add, ALU.mult, [bk, ("vecT",)] + ck, [("D", j, 16)])
            stt(GF32[:, j, 32:160], pb, bga, SGB[:, j, 2048:2176], ALU.add, ALU.mult, [bk, ("vecT",)] + ck, [("S", "gf", j, 1)])
    proj_fm(wi3[:, :, 2048:3072], KC, hT, hkey, 8, evac_glu, sts=ST512)
    pt = PS[2].rearrange("p (c t) -> p c t", c=KC)
    for j in range(KC):
        tp(pt[0:30, j, :], GF32[:, j, 2:32], 128, [("S", "gf", j, 0)], [("ps", 4 + j // 4)])
    cp("dve", STG[0][0:30, :], PS[2][0:30, :], [("ps", 4), ("ps", 5)], [("S", "stg", 0)])
    dma("sp", ncp_out, STG[0][0:30, :], [("S", "stg", 0)], [("out", "ncp")], ("ncp",))
    OUTK.append(("out", "ncp"))
    pt = PS[3].rearrange("p (c t) -> p c t", c=KC)
    for j in range(KC):
        tp(pt[:, j, :], GF32[:, j, 32:160], 128, [("S", "gf", j, 1)], [("ps", 6 + j // 4)])
    cp("dve", STG[1], PS[3], [("ps", 6), ("ps", 7)], [("S", "stg", 1)])
    R.group_keys.add(("ncs",))
    for q in range(16):
        dma("sp", ncs_out[q, 22:30, :], STG[1][q * 8:(q + 1) * 8, :], [("S", "stg", 1)], [("out", "ncs", q)], ("ncs",))
        OUTK.append(("out", "ncs", q))
    dma("sp", ncs_out[:, 0:22, :], cache[:, 8:30, :], [], [("out", "ncs0")], ("ncs",))
    OUTK.append(("out", "ncs0"))
    dump("GP", GP[:, :, 0:2078], [128, KC, 2078], [("D", "hist")] + [("D", j, tt) for j in range(KC) for tt in range(16)])
    dump("GS", GS, [128, KC, 16, 38], [("D", "gsh", g) for g in range(4)] + [("D", j, 16) for j in range(KC)])

    R.fence("C")
    SB = U
    proj_fm(wi3[:, :, 5120:6144], KC, hT, hkey, 8, evac_sig(SB, "C", R_BIN + 5))

    R.fence("S")
    R.fence("A")
    CV = M.at("CV", [128, KC, T], BF16, OFF_A)
    DG = [M.at(f"DG{i}", [128, 31, 128], BF16, OFF_S + i * 7936) for i in range(2)]
    slot_pb, wv_pb = wload(w_pb.rearrange("(k p) n -> p k n", p=128), KC, 1024)
    NPE = 26

    def build_dg(c):
        tt_("dve", DG[c % 2], identb.unsqueeze(1).to_broadcast([128, 31, 128]),
            vecT[:, c, R_CW:R_CW + 31].unsqueeze(2).to_broadcast([128, 31, 128]), ALU.mult, [("identb",), ("vecT",)], [("S", "dg", c % 2)])
    build_dg(0)
    build_dg(1)
    for c in range(KC):
        dg = DG[c % 2]
        kd = ("S", "dg", c % 2)
        for (t0, w) in ST512:
            b = next_bank()
            pb = bank(b)[:, 0:w]
            if t0 < 2048:
                rk = [kd] + [("D", c, tt) for tt in range(max(0, (t0 - 30) // 128), (t0 + w - 1) // 128 + 1)]
                if t0 == 0:
                    rk.append(("D", "hist"))
                for k in range(NPE):
                    mm(pb, dg[:, k, :], GP[:, c, t0 + k:t0 + k + w], k == 0, k == NPE - 1, rk, [("ps", b)])
                for k in range(NPE, 31):
                    stt(pb, GP[:, c, t0 + k:t0 + k + w], vecT[:, c, R_CW + k:R_CW + k + 1], pb, ALU.mult, ALU.add,
                        rk[1:] + [("ps", b), ("vecT",)], [("ps", b)])
            else:
                rk = [kd, ("D", c, 16)] + [("D", "gsh", g) for g in range(4)]
                for k in range(NPE):
                    mm(pb, dg[:, k, :], GS[:, c, :, k:k + 8], k == 0, k == NPE - 1, rk, [("ps", b)])
                pb3 = pb.rearrange("p (q b) -> p q b", b=8)
                for k in range(NPE, 31):
                    stt(pb3, GS[:, c, :, k:k + 8], vecT[:, c, R_CW + k:R_CW + k + 1], pb3, ALU.mult, ALU.add,
                        rk[1:] + [("ps", b), ("vecT",)], [("ps", b)])
            act(CV[:, c, t0:t0 + w], pb, AF.Identity, [("ps", b), ("vecT",)], [("A", c, tt) for tt in tts(t0, w)],
                bias=vecT[:, c, R_CB:R_CB + 1])
        if c + 2 < KC:
            build_dg(c + 2)
    dump("CV", CV, [128, KC, T], [("A", j, tt) for j in range(KC) for tt in range(NTT)])

    R.fence("S")
    NF = [M.at(f"NF{i}", [128, D], F32, OFF_S + i * 4096) for i in range(2)]
    VEPS = M.at("veps", [128, 32], F32, OFF_S + 8192)

    def ln_stages(tt):
        pa = PS[2]
        pa3 = pa.rearrange("p (c t) -> p c t", c=KC)
        ka = [("ps", 4), ("ps", 5)]
        pcp = 3 if tt % 2 == 0 else 1
        pc = PS[pcp].rearrange("p (c t) -> p c t", c=KC)
        kc_ = [("ps", 2 * pcp), ("ps", 2 * pcp + 1)]
        nf = NF[tt % 2]
        knf = ("S", "nf", tt % 2)
        col = newcol()

        def sa():
            for c in range(KC):
                mm(pa3[:, c, :], CV[:, c, tt * 128:(tt + 1) * 128], identb, True, True, [("A", c, tt), ("identb",)], [ka[c // 4]])
            for h in range(2):
                act(nf[:, h * 512:(h + 1) * 512], pa[:, h * 512:(h + 1) * 512], AF.Identity, [ka[h]],
                    [knf])

        def sb():
            for h in range(2):
                R.op("dve", (lambda o2, i2: (lambda e: e.bn_stats(out=o2, in_=i2)))(st6[:, tt % 2, h, :], nf[:, h * 512:(h + 1) * 512]),
                     reads=[knf], writes=[("st6", tt % 2, h)])
            R.op("dve", (lambda o2, i2: (lambda e: e.bn_aggr(out=o2, in_=i2)))(mv[:, tt % 2, :], st6[:, tt % 2, :, :].rearrange("p h s -> p (h s)")),
                 reads=[("st6", tt % 2, 0), ("st6", tt % 2, 1)], writes=[("mv", tt % 2)])
            ts_("dve", VEPS[:, tt:tt + 1], mv[:, tt % 2, 1:2], EPS, None, ALU.add, None, [("mv", tt % 2)], [("S", "veps", tt)])
            tt_("pool", rstd[:, col:col + 1], VEPS[:, tt:tt + 1], negh, ALU.pow, [("S", "veps", tt), ("negh",)], [("rstd", col)])

        def sc():
            ts_("dve", nf, nf, mv[:, tt % 2, 0:1], rstd[:, col:col + 1], ALU.subtract, ALU.mult, [knf, ("mv", tt % 2), ("rstd", col)], [knf])
            for c in range(KC):
                tp(pc[:, c, :], nf[:, c * 128:(c + 1) * 128], 128, [knf], [kc_[c // 4]])

        def sd():
            for c in range(KC):
                act(CV[:, c, tt * 128:(tt + 1) * 128], pc[:, c, :], AF.Silu, [kc_[c // 4], ("vecT",)], [("A", c, tt)],
                    bias=vecT[:, c, R_LCB:R_LCB + 1], scale=vecT[:, c, R_LCG:R_LCG + 1])
        return sa, sb, sc, sd

    def evac_m(j, t0, w, pb, bk):
        kb = [("B", j, tt) for tt in tts(t0, w)]
        kc = [("C", j, tt) for tt in tts(t0, w)]
        stt(pb, pb, vecT[:, j, R_BPB:R_BPB + 1], SB[:, j, t0:t0 + w], ALU.add, ALU.mult, [bk, ("vecT",)] + kc, [bk])
        tt_("dve", SA[:, j, t0:t0 + w], pb, SA[:, j, t0:t0 + w], ALU.add, [bk] + kb, kb)

    ckey = lambda k, tt: ("A", k, tt)
    lng = skewed([ln_stages(tt) for tt in range(NTT)], reverse=True)
    pend = []
    done_st = 0
    jb = {"n": 0}
    for step in range(NTT + 3 + 64):
        cur = []
        for _ in range(2):
            if pend:
                j, t0, w = pend.pop(0)
                b = jb["n"] % 2
                jb["n"] += 1
                pb = bank(b)[:, 0:w]
                for k in range(KC):
                    rk = [("W", slot_pb, 0), ("W", slot_pb, 1)] + [("A", k, tt) for tt in tts(t0, w)]
                    mm(pb, wv_pb[:, k, j * 128:(j + 1) * 128], CV[:, k, t0:t0 + w], k == 0, k == KC - 1, rk, [("ps", b)])
                cur.append((j, t0, w, pb, ("ps", b)))
        alive = next(lng, "end") != "end"
        ndone = step - 3 + 1
        while done_st < len(STMIX) and ndone * 128 >= STMIX[done_st][0] + STMIX[done_st][1]:
            pend.extend((j, STMIX[done_st][0], STMIX[done_st][1]) for j in range(KC))
            done_st += 1
        for (j, t0, w, pb, bk) in cur:
            evac_m(j, t0, w, pb, bk)
        if not alive and not pend and done_st == len(STMIX):
            break
    dump("S", CV, [128, KC, T], [("A", j, tt) for j in range(KC) for tt in range(NTT)])
    dump("M", SA, [128, KC, T], [("B", j, tt) for j in range(KC) for tt in range(NTT)])

    def build_gt(mi, dst, dkey, gx, kgx):
        for gi in range(2):
            ttx = 0 if gi == 0 else 16
            cp("dve", tokv(gx, ttx), modv(mi, ttx), [("modT", mi)], [kgx])
            pr = 2 + gi
            pt_ = PS[pr].rearrange("p (c t) -> p c t", c=KC)
            for c in range(KC):
                tp(pt_[:, c, :], gx[:, c, :], 128, [kgx], [("ps", 2 * pr + c // 4)])
            cp("dve", dst[:, gi, :], PS[pr], [("ps", 2 * pr), ("ps", 2 * pr + 1)], [(dkey[0], dkey[1], gi)])

    R.fence("S")
    R.fence("D")
    XIN2 = [M.at(f"xin2_{i}", [128, D], F32, OFF_S + i * 4096) for i in range(3)]
    XNB = [M.at(f"xnb{i}", [128, D], F32, OFF_S + 12288 + i * 4096) for i in range(2)]
    SQJ2 = M.at("sqj2", [128, D], BF16, OFF_S + 20480)
    GXv = XNB[0].rearrange("p (c t) -> p c t", c=KC)
    GT1 = M.at("GT1", [128, 2, D], F32, OFF_D)
    h2T = M.at("h2T", [128, KC, 1152], BF16, OFF_B + 59392)
    ACTB = M.at("ACTB", [128, FC, 1152], BF16, OFF_B)
    GT2 = M.at("GT2", [128, 2, D], F32, OFF_B + 50688)
    build_gt(M_GT1, GT1, ("D", "gt1"), GXv, ("S", "xn", 0))
    R.fence_into("X", ["A", "C"])
    X1 = M.at("X1", [128, NTT, D], F32, OFF_A)
    GROUPS = (list(range(0, 9)), list(range(9, 17)))

    def norm2_stages(i, tt):
        return rmsnorm_stages(X1[:, tt, :], [("X", tt, q) for q in range(4)], XNB[i % 2], ("S", "xn", i % 2), SQJ2, 2 + i % 2, tt,
                              h2T[:, :, i * 128:(i + 1) * 128], [("D", "h", i)], M_G2, M_SH2)

    slot, wv = wload(w_o.rearrange("(k p) n -> p k n", p=128), KC, 1024)
    for tt in range(NTT):
        s_ = tt % 3
        dma("sp", XIN2[s_], xc[tt * 128:(tt + 1) * 128, :], [], [("S", "xin", s_)], ("xin", s_))
        p = next_pair()
        pp = PS[p]
        pk = [("ps", 2 * p), ("ps", 2 * p + 1)]
        for h in range(2):
            for k in range(KC):
                mm(pp[:, h * 512:(h + 1) * 512], SA[:, k, tt * 128:(tt + 1) * 128], wv[:, k, h * 512:(h + 1) * 512], k == 0, k == KC - 1,
                   [("B", k, tt), ("W", slot, 0), ("W", slot, 1)], [pk[h]])
        gi = 0 if tt < 16 else 1
        for h in range(2):
            hs = slice(h * 512, (h + 1) * 512)
            tt_("dve", pp[:, hs], pp[:, hs], GT1[:, gi, hs], ALU.mult, [pk[h], ("D", "gt1", gi)], [pk[h]])
        for h in range(2):
            hs = slice(h * 512, (h + 1) * 512)
            tt_("dve", X1[:, tt, hs], pp[:, hs], XIN2[s_][:, hs], ALU.add, [pk[h], ("S", "xin", s_)], [("X", tt, 2 * h), ("X", tt, 2 * h + 1)])
        if tt == 1:
            g0gen = skewed([norm2_stages(i, t_) for i, t_ in enumerate(GROUPS[0])])
        nst_ = len(GROUPS[0]) + 2
        if tt >= 1 and (tt * nst_) // 16 > ((tt - 1) * nst_) // 16:
            next(g0gen, None)
    drain(g0gen)
    dump("X1", X1, [128, NTT, D], [("X", tt, q) for tt in range(NTT) for q in range(4)])

    R.fence_into("F", ["B", "D"])
    R.fence("S")
    SG = [M.at(f"SG{i}", [128, 512], F32, OFF_S + i * 2048) for i in range(2)]
    GFB = M.at("GFB", [128, D], F32, OFF_S + 4096)
    dma("sp", GFB, g_final.partition_broadcast(128), [], [("S", "gfb")], ("m", 2))
    build_gt(M_GT2, GT2, ("F", "gt2"), GXv, ("S", "xn", 0))
    wf3 = w_ffn_in.rearrange("(k p) n -> p k n", p=128)
    wo3 = w_ffn_out.rearrange("(j p) n -> p j n", p=128)
    R.group_keys.add(("yout",))
    sgc = {"n": 0}

    gfb = {"ap": GFB, "keys": [("S", "gfb")]}

    def final_tile(tt):
        col = newcol()
        xk = [("X", tt, q) for q in range(4)]
        act(SQJ2, X1[:, tt, :], AF.Square, xk, [("S", "sqj"), ("ssum", col)], accum_out=ssum[:, col:col + 1])
        act(rstd[:, col:col + 1], ssum[:, col:col + 1], AF.Sqrt, [("ssum", col), ("epsT",)], [("rstd", col)], bias=epsT[:, 0:1], scale=1.0 / D)

        def part_b():
            R.op("dve", lambda e: e.reciprocal(out=rstd[:, col:col + 1], in_=rstd[:, col:col + 1]), reads=[("rstd", col)], writes=[("rstd", col)])
            stt(X1[:, tt, :], X1[:, tt, :], rstd[:, col:col + 1], gfb["ap"], ALU.mult, ALU.mult, xk + [("rstd", col)] + gfb["keys"], xk)
            dma("sp", y_out[tt * 128:(tt + 1) * 128, :], X1[:, tt, :], xk, [("out", "y", tt)], ("yout",))
            OUTK.append(("out", "y", tt))
        return part_b

    def gen_ffn_in(tiles):
        nloc = len(tiles) * 128
        lst = [(0, 384), (384, 384), (768, 384)] if nloc == 1152 else [(l0, min(512, nloc - l0)) for l0 in range(0, nloc, 512)]
        for wl in range(6):
            j0 = wl * 4
            nj = min(4, FC - j0)
            slot = wstate["n"] % 2
            wstate["n"] += 1
            wv = wview(slot, KC, 1024)
            dma("pool", wv[:, :, 0:nj * 128], wf3[:, :, j0 * 128:(j0 + nj) * 128], [], [("W", slot, 0)], ("W", slot, 0))
            dma("pool", wv[:, :, 512:512 + nj * 128], wf3[:, :, DFF + j0 * 128:DFF + (j0 + nj) * 128], [], [("W", slot, 1)], ("W", slot, 1))
            for jj in range(nj):
                j = j0 + jj
                for (l0, w) in lst:
                    bg = next_bank()
                    bu = next_bank()
                    hk = [("D", "h", i) for i in tts(l0, w)]
                    for k in range(KC):
                        mm(bank(bg)[:, 0:w], wv[:, k, jj * 128:(jj + 1) * 128], h2T[:, k, l0:l0 + w], k == 0, k == KC - 1,
                           [("W", slot, 0)] + hk, [("ps", bg)])
                    for k in range(KC):
                        mm(bank(bu)[:, 0:w], wv[:, k, 512 + jj * 128:512 + (jj + 1) * 128], h2T[:, k, l0:l0 + w], k == 0, k == KC - 1,
                           [("W", slot, 1)] + hk, [("ps", bu)])
                    sg = SG[sgc["n"] % 2]
                    ksg = ("S", "sg", sgc["n"] % 2)
                    sgc["n"] += 1
                    act(sg[:, 0:w], bank(bg)[:, 0:w], AF.Silu, [("ps", bg)], [ksg])
                    tt_("dve", ACTB[:, j, l0:l0 + w], bank(bu)[:, 0:w], sg[:, 0:w], ALU.mult, [("ps", bu), ksg],
                        [("F", "a", j, i) for i in tts(l0, w)])
                    yield

    defer = [None]

    def gen_ffn_out(tiles):
        for q in range(4):
            slot, wv = wload(wo3[:, :, q * 256:(q + 1) * 256], FC, 256)
            for i, tt in enumerate(tiles):
                b = next_bank()
                pb = bank(b)[:, 0:256]
                for j in range(FC):
                    mm(pb, ACTB[:, j, i * 128:(i + 1) * 128], wv[:, j, :], j == 0, j == FC - 1,
                       [("F", "a", j, i), ("W", slot, 0), ("W", slot, 1)], [("ps", b)])
                gi = 0 if tt < 16 else 1
                xq = X1[:, tt, q * 256:(q + 1) * 256]
                tt_("dve", pb, pb, GT2[:, gi, q * 256:(q + 1) * 256], ALU.mult, [("ps", b), ("F", "gt2", gi)], [("ps", b)])
                tt_("dve", xq, pb, xq, ALU.add, [("ps", b), ("X", tt, q)], [("X", tt, q)])
                if q == 3:
                    pb_new = final_tile(tt)
                    if defer[0] is not None:
                        defer[0]()
                    defer[0] = pb_new
                yield
        if defer[0] is not None:
            defer[0]()
            defer[0] = None

    def gen_norm2(tiles):
        return skewed([norm2_stages(i, t_) for i, t_ in enumerate(tiles)])

    drain(gen_ffn_in(GROUPS[0]))
    interleave(gen_ffn_out(GROUPS[0]), 4 * len(GROUPS[0]), gen_norm2(GROUPS[1]), (len(GROUPS[1]) + 2) * 2)
    cp("dve", PS[3], GFB, [("S", "gfb")], [("ps", 6), ("ps", 7)])
    gfb["ap"] = PS[3]
    gfb["keys"] = [("ps", 6), ("ps", 7)]
    drain(gen_ffn_in(GROUPS[1]))
    drain(gen_ffn_out(GROUPS[1]))

    out_keys = [("dbgout", n) for n in dbg_outs] + OUTK
    R.op("sp", lambda e: e.nop(), reads=out_keys, writes=[])
    nops, nsem = R.emit()
    return nc, dbg_outs, (nops, nsem)


_CACHE = {}


def _prep_inputs(inputs, b):
    f = lambda a: np.ascontiguousarray(np.asarray(a, dtype=np.float32))
    xs = f(inputs["x_sample"])[16 * b:16 * b + 16].reshape(128, D)
    m = {
        "xc": np.concatenate([f(inputs["x_prompt"])[b], xs], axis=0),
        "cc": np.concatenate([f(inputs["c_sample"])[16 * b:16 * b + 16], f(inputs["c_prompt"])[b:b + 1]], axis=0),
        "cache": f(inputs["cache_conv"])[0, 16 * b:16 * b + 16],
        "w_ada": f(inputs["w_ada"])[0], "b_ada": f(inputs["b_ada"])[0].reshape(6, D),
        "g_norm1": f(inputs["g_norm1"]), "w_in": f(inputs["w_in"])[0], "b_in": f(inputs["b_in"])[0].reshape(6, D),
        "ln_v_g": f(inputs["ln_v_g"]), "ln_v_b": f(inputs["ln_v_b"]),
        "w_s": f(inputs["w_s"])[0], "b_s": f(inputs["b_s"])[0].reshape(1, 512),
        "w_pa": f(inputs["w_pa"])[0], "b_pa": f(inputs["b_pa"]),
        "conv_w": f(inputs["conv_w"])[0], "conv_b": f(inputs["conv_b"]),
        "ln_c_g": f(inputs["ln_c_g"]), "ln_c_b": f(inputs["ln_c_b"]),
        "w_pb": f(inputs["w_pb"])[0], "b_pb": f(inputs["b_pb"]),
        "w_o": f(inputs["w_o"])[0], "g_norm2": f(inputs["g_norm2"]),
        "w_ffn_in": f(inputs["w_ffn_in"])[0], "w_ffn_out": f(inputs["w_ffn_out"])[0],
        "g_final": f(inputs["g_final"]).reshape(1, D),
    }
    return {k: np.ascontiguousarray(v) for k, v in m.items()}


def kernel(**inputs):
    if "nc" not in _CACHE:
        _CACHE["nc"] = build()[0]
    nc = _CACHE["nc"]
    in_maps = [_prep_inputs(inputs, b) for b in range(NCORES)]
    res = run_bass_kernel_spmd(nc, in_maps, core_ids=list(range(NCORES)))
    rs = res.results
    y_prompt = np.stack([r["y"][:2048] for r in rs]).astype(np.float32)
    y_sample = np.concatenate([r["y"][2048:].reshape(16, 8, D) for r in rs]).astype(np.float32)
    ncp = np.stack([r["ncp"] for r in rs])[None].astype(np.float32)
    ncs = np.concatenate([r["ncs"] for r in rs])[None].astype(np.float32)
    cvp = np.stack([r["cvp"] for r in rs])[None].astype(np.float32)
    cvs = np.concatenate([r["cvs"].reshape(16, 8, D) for r in rs])[None].astype(np.float32)
    return (y_prompt, y_sample, ncp, ncs, cvp, cvs)
```

```python
import numpy as np
import concourse.bass as bass
import concourse.mybir as mybir
from concourse.bass_utils import run_bass_kernel_spmd

F32 = mybir.dt.float32
BF16 = mybir.dt.bfloat16
AF = mybir.ActivationFunctionType
ALU = mybir.AluOpType

D = 1024
KC = 8
T = 2176
NTT = 17
DFF = 2816
FC = 22
EPS = 1e-6
NCORES = 8
SB_BASE = 16512
SB_LIMIT = 229376

R_BADA, R_BIN, R_G1, R_LVG, R_LVB, R_BPA, R_CB, R_LCG, R_LCB, R_BPB, R_G2, R_CW = 0, 6, 12, 13, 14, 15, 16, 17, 18, 19, 20, 21
NVEC = 52
M_SH1, M_G1, M_GT1, M_SH2, M_G2, M_GT2 = 0, 1, 2, 3, 4, 5


class _Op:
    __slots__ = ("eng", "fn", "deps_eng", "deps_dma", "dma_key", "needs_inc", "inc_val", "sem_val")

    def __init__(self, eng, fn, deps_eng, deps_dma, dma_key):
        self.eng = eng
        self.fn = fn
        self.deps_eng = deps_eng
        self.deps_dma = deps_dma
        self.dma_key = dma_key
        self.needs_inc = False
        self.inc_val = 0
        self.sem_val = 0


class Rec:
    def __init__(self, nc):
        self.nc = nc
        self.ops = []
        self.state = {}
        self.fences = {}
        self.group_keys = set()

    def _add_dep(self, de, dd, idx):
        if idx is None:
            return
        o = self.ops[idx]
        if o.dma_key is not None:
            dd.add(idx)
        else:
            if de.get(o.eng, -1) < idx:
                de[o.eng] = idx

    def op(self, eng, fn, reads=(), writes=(), dma_key=None):
        idx = len(self.ops)
        de, dd = {}, set()
        for k in list(reads) + list(writes):
            if k not in self.state:
                f = self.fences.get(k[0])
                if f is not None:
                    for e, i in f[0].items():
                        if de.get(e, -1) < i:
                            de[e] = i
                    dd |= f[1]
        for k in reads:
            st = self.state.get(k)
            if st is not None:
                self._add_dep(de, dd, st[0])
        for k in writes:
            st = self.state.get(k)
            if st is not None:
                self._add_dep(de, dd, st[0])
                for e, i in st[1].items():
                    if de.get(e, -1) < i:
                        de[e] = i
                dd |= st[2]
        self.ops.append(_Op(eng, fn, de, dd, dma_key))
        is_dma = dma_key is not None
        for k in reads:
            st = self.state.setdefault(k, [None, {}, set()])
            if is_dma:
                st[2].add(idx)
            else:
                st[1][eng] = idx
        for k in writes:
            self.state[k] = [idx, {}, set()]
        return idx

    def fence_into(self, new_region, olds):
        de, dd = {}, set()
        for r in olds:
            self.fence(r)
            f = self.fences[r]
            for e, i in f[0].items():
                if de.get(e, -1) < i:
                    de[e] = i
            dd |= f[1]
        old = self.fences.get(new_region)
        if old is not None:
            for e, i in old[0].items():
                if de.get(e, -1) < i:
                    de[e] = i
            dd |= old[1]
        self.fence(new_region)
        f = self.fences[new_region]
        for e, i in f[0].items():
            if de.get(e, -1) < i:
                de[e] = i
        dd |= f[1]
        self.fences[new_region] = (de, dd)

    def fence(self, region):
        de, dd = {}, set()
        old = self.fences.get(region)
        if old is not None:
            de.update(old[0])
            dd |= old[1]
        for k in [k for k in self.state if k[0] == region]:
            st = self.state.pop(k)
            self._add_dep(de, dd, st[0])
            for e, i in st[1].items():
                if de.get(e, -1) < i:
                    de[e] = i
            dd |= st[2]
        self.fences[region] = (de, dd)

    def emit(self):
        nc = self.nc
        engs = {"pe": nc.tensor, "act": nc.scalar, "dve": nc.vector, "pool": nc.gpsimd, "sp": nc.sync}
        ops = self.ops
        for o in ops:
            for e, i in o.deps_eng.items():
                if e == "pe" and o.eng == "pe" and o.dma_key is None:
                    continue
                ops[i].needs_inc = True
        cnt = {e: 0 for e in engs}
        dma_cnt = {}
        for o in ops:
            if o.dma_key is not None:
                dma_cnt[o.dma_key] = dma_cnt.get(o.dma_key, 0) + 16
                o.sem_val = dma_cnt[o.dma_key]
            elif o.needs_inc:
                cnt[o.eng] += 1
                o.inc_val = cnt[o.eng]
        for o in ops:
            if o.dma_key in self.group_keys:
                o.sem_val = dma_cnt[o.dma_key]
        esem = {e: nc.alloc_semaphore("s_" + e) for e in engs}
        dsem = {}
        for k in dma_cnt:
            dsem[k] = nc.alloc_semaphore("d_" + "_".join(str(x) for x in k))
        waited = {}
        for o in ops:
            e = engs[o.eng]
            waits = {}
            for de, i in o.deps_eng.items():
                if de == "pe" and o.eng == "pe" and o.dma_key is None:
                    continue
                s = esem[de]
                v = ops[i].inc_val
                if waits.get(s, (0, 0))[1] < v:
                    waits[s] = (s, v)
            for i in o.deps_dma:
                s = dsem[ops[i].dma_key]
                v = ops[i].sem_val
                if waits.get(s, (0, 0))[1] < v:
                    waits[s] = (s, v)
            for s, v in waits.values():
                kk = (o.eng, s.num if hasattr(s, "num") else id(s))
                if waited.get(kk, 0) < v:
                    e.wait_ge(s, v)
                    waited[kk] = v
            ins = o.fn(e)
            if o.dma_key is not None:
                ins.then_inc(dsem[o.dma_key], 16)
            elif o.needs_inc:
                ins.then_inc(esem[o.eng], 1)
        return len(ops), len(dsem) + len(esem)


class Mem:
    def __init__(self, nc):
        self.nc = nc
        self.n = 0

    def at(self, name, shape, dtype, off):
        assert off % 32 == 0, (name, off)
        nbytes = int(np.prod(shape[1:])) * (2 if dtype == BF16 else 4)
        assert SB_BASE + off + nbytes <= SB_LIMIT, (name, off, nbytes)
        self.n += 1
        return self.nc.alloc_sbuf_tensor_at(f"{name}_{self.n}", list(shape), dtype, offset=SB_BASE + off).ap()


def build(dbg=()):
    nc = bass.Bass("TRN2", target_bir_lowering=False)
    R = Rec(nc)
    M = Mem(nc)
    dbg = set(dbg)
    dbg_outs = {}

    def din(name, shape):
        return nc.dram_tensor(name, list(shape), F32, kind="ExternalInput").ap()

    def dout(name, shape):
        return nc.dram_tensor(name, list(shape), F32, kind="ExternalOutput").ap()

    xc = din("xc", [T, D])
    cc = din("cc", [17, D])
    cache = din("cache", [16, 30, D])
    w_ada = din("w_ada", [D, 6 * D])
    b_ada = din("b_ada", [6, D])
    g_norm1 = din("g_norm1", [1, D])
    w_in = din("w_in", [D, 6 * D])
    b_in = din("b_in", [6, D])
    ln_v_g = din("ln_v_g", [1, D])
    ln_v_b = din("ln_v_b", [1, D])
    w_s = din("w_s", [4, 128, 128])
    b_s = din("b_s", [1, 512])
    w_pa = din("w_pa", [D, D])
    b_pa = din("b_pa", [1, D])
    conv_w = din("conv_w", [31, D])
    conv_b = din("conv_b", [1, D])
    ln_c_g = din("ln_c_g", [1, D])
    ln_c_b = din("ln_c_b", [1, D])
    w_pb = din("w_pb", [D, D])
    b_pb = din("b_pb", [1, D])
    w_o = din("w_o", [D, D])
    g_norm2 = din("g_norm2", [1, D])
    w_ffn_in = din("w_ffn_in", [D, 2 * DFF])
    w_ffn_out = din("w_ffn_out", [DFF, D])
    g_final = din("g_final", [1, D])

    y_out = dout("y", [T, D])
    ncp_out = dout("ncp", [30, D])
    ncs_out = dout("ncs", [16, 30, D])
    cvp_out = dout("cvp", [128, D])
    cvs_out = dout("cvs", [128, D])

    o = 0
    ident32 = M.at("ident32", [128, 128], F32, o); o += 512
    onesb = M.at("onesb", [128, 128], BF16, o); o += 256
    onesm = M.at("onesm", [128, 128], BF16, o); o += 256
    vecT = M.at("vecT", [128, KC, NVEC], F32, o); o += KC * NVEC * 4
    modT = M.at("modT", [128, 48, 17], F32, o); o += 48 * 17 * 4
    cT = M.at("cT", [128, KC, 17], BF16, o); o += 288
    WT = M.at("WT", [128, 4, 128], BF16, o); o += 1024
    WTS = M.at("WTS", [128, 4, 128], BF16, o); o += 1024
    ssum = M.at("ssum", [128, 64], F32, o); o += 256
    rstd = M.at("rstd", [128, 64], F32, o); o += 256
    st6 = M.at("st6", [128, 2, 2, 6], F32, o); o += 96
    mv = M.at("mv", [128, 2, 2], F32, o); o += 32
    epsT = M.at("epsT", [128, 1], F32, o); o += 32
    identb = M.at("identb", [128, 128], BF16, o); o += 256
    negh = M.at("negh", [128, 1], F32, o); o += 32
    o = (o + 255) // 256 * 256
    CONST_END = o
    WOFF = [o, o + 16384]
    o += 32768
    OFF_A = o; o += 34816
    OFF_C = o; o += 34816
    OFF_B = o; o += 34816
    OFF_D = o; o += 43008
    OFF_S = o
    SCR_SIZE = (SB_LIMIT - SB_BASE) - OFF_S
    assert SCR_SIZE >= 22528, SCR_SIZE

    Wb = [M.at(f"W{i}", [128, 8192], BF16, WOFF[i]) for i in range(2)]

    def wview(slot, kch, ncols):
        return Wb[slot][:, 0:kch * ncols].rearrange("p (k n) -> p k n", k=kch)

    PS = [nc.alloc_psum_tensor(f"ps{i}", [128, 1024], F32).ap() for i in range(4)]

    def bank(i):
        return PS[i // 2][:, (i % 2) * 512:(i % 2) * 512 + 512]

    hT = M.at("hT", [128, KC, T], BF16, OFF_A)
    nB = M.at("nB", [128, NTT, D], BF16, OFF_B)
    U = M.at("U", [128, KC, T], BF16, OFF_C)
    SA = M.at("SA", [128, KC, T], BF16, OFF_B)

    ST512 = [(0, 512), (512, 512), (1024, 512), (1536, 512), (2048, 128)]
    STMIX = [(0, 512), (512, 512), (1024, 384), (1408, 384), (1792, 384)]

    def tts(t0, w):
        return range(t0 // 128, (t0 + w + 127) // 128)

    def dma(eng, out, in_, reads, writes, key):
        R.op(eng, lambda e: e.dma_start(out=out, in_=in_), reads=reads, writes=writes, dma_key=key)

    def mm(out, lhsT, rhs, start, stop, reads, writes):
        R.op("pe", lambda e: e.matmul(out, lhsT=lhsT, rhs=rhs, start=start, stop=stop), reads=reads, writes=writes)

    def tp(out, in_, nrows, reads, writes):
        R.op("pe", lambda e: e.transpose(out, in_, ident32[0:nrows, 0:nrows]), reads=list(reads) + [("ident",)], writes=writes)

    def act(out, in_, func, reads, writes, bias=None, scale=None, accum_out=None):
        kw = {}
        if bias is not None:
            kw["bias"] = bias
        if scale is not None:
            kw["scale"] = scale
        if accum_out is not None:
            kw["accum_out"] = accum_out
        R.op("act", lambda e: e.activation(out=out, in_=in_, func=func, **kw), reads=reads, writes=writes)

    def tt_(eng, out, in0, in1, op, reads, writes):
        R.op(eng, lambda e: e.tensor_tensor(out=out, in0=in0, in1=in1, op=op), reads=reads, writes=writes)

    def ts_(eng, out, in0, s1, s2, op0, op1, reads, writes):
        if s2 is None:
            R.op(eng, lambda e: e.tensor_scalar(out=out, in0=in0, scalar1=s1, scalar2=None, op0=op0), reads=reads, writes=writes)
        else:
            R.op(eng, lambda e: e.tensor_scalar(out=out, in0=in0, scalar1=s1, scalar2=s2, op0=op0, op1=op1), reads=reads, writes=writes)

    def stt(out, in0, scalar, in1, op0, op1, reads, writes):
        R.op("dve", lambda e: e.scalar_tensor_tensor(out=out, in0=in0, scalar=scalar, in1=in1, op0=op0, op1=op1), reads=reads, writes=writes)

    def cp(eng, out, in_, reads, writes):
        R.op(eng, lambda e: e.tensor_copy(out=out, in_=in_), reads=reads, writes=writes)

    def dump(name, ap, shape, reads):
        if name not in dbg:
            return
        d = dout("dbg_" + name, shape)
        dbg_outs[name] = shape
        dma("sp" if ap.dtype == F32 else "pool", d, ap, reads, [("dbgout", name)], ("dbg", name))

    wstate = {"n": 0}

    def wload(src3, kch, ncols, extra_reads=()):
        slot = wstate["n"] % 2
        wstate["n"] += 1
        v = wview(slot, kch, ncols)
        dma("pool", v, src3, list(extra_reads), [("W", slot, 0), ("W", slot, 1)], ("W", slot, 0))
        return slot, v

    psrot = {"n": 0}

    def next_bank():
        b = psrot["n"] % 4
        psrot["n"] += 1
        return b

    def next_pair():
        psrot["n"] = (psrot["n"] + 1) // 2 * 2
        p = (psrot["n"] // 2) % 2
        psrot["n"] += 2
        return p

    R.op("pool", lambda e: e.memset(ident32, 1.0), writes=[("ident",)])
    R.op("pool", lambda e: e.affine_select(out=ident32, in_=ident32, pattern=[[-1, 128]], compare_op=ALU.is_equal,
                                           fill=0.0, base=0, channel_multiplier=1), reads=[("ident",)], writes=[("ident",)])
    R.op("dve", lambda e: e.memset(onesb, 1.0), writes=[("onesb",)])
    R.op("dve", lambda e: e.memset(onesm, 1.0 / 1024.0), writes=[("onesm",)])
    R.op("dve", lambda e: e.memset(epsT, EPS), writes=[("epsT",)])
    cp("dve", identb, ident32, [("ident",)], [("identb",)])
    R.op("dve", lambda e: e.memset(negh, -0.5), writes=[("negh",)])

    wa3 = w_ada.rearrange("(k p) n -> p k n", p=128)
    ADA = [M.at(f"ADA{i}", [128, KC, 1024], BF16, OFF_D + i * 16384) for i in range(2)]

    def ada_load(blk, extra_reads=(), buf=None):
        bi = blk % 2 if buf is None else buf
        dma("pool", ADA[bi], wa3[:, :, blk * 1024:(blk + 1) * 1024], list(extra_reads), [("D", "ada", bi)], ("ada", bi))

    def ada_compute(blk, buf=None, rot=False):
        bi = blk % 2 if buf is None else buf
        wv_ = ADA[bi]
        bk = next_bank() if rot else 4 + (blk % 2)
        pm = bank(bk)[:, 0:KC * 17].rearrange("p (j s) -> p j s", j=KC)
        for j in range(KC):
            for k in range(KC):
                mm(pm[:, j, :], wv_[:, k, j * 128:(j + 1) * 128], cT[:, k, :], k == 0, k == KC - 1,
                   [("D", "ada", bi), ("cT",)], [("ps", bk)])
        tt_("dve", modT[:, blk * 8:(blk + 1) * 8, :], pm, vecT[:, :, R_BADA + blk:R_BADA + blk + 1].to_broadcast([128, KC, 17]),
            ALU.add, [("ps", bk), ("vecT",)], [("modT", blk)])
        if blk in (M_G1, M_G2):
            gr = R_G1 if blk == M_G1 else R_G2
            ts_("dve", modT[:, blk * 8:(blk + 1) * 8, :], modT[:, blk * 8:(blk + 1) * 8, :], 1.0, None, ALU.add, None,
                [("modT", blk)], [("modT", blk)])
            tt_("dve", modT[:, blk * 8:(blk + 1) * 8, :], modT[:, blk * 8:(blk + 1) * 8, :],
                vecT[:, :, gr:gr + 1].to_broadcast([128, KC, 17]), ALU.mult, [("modT", blk), ("vecT",)], [("modT", blk)])

    BV2 = M.at("BV2", [2, D], BF16, OFF_D + 40960)
    bvH = BV2[0:1, :]
    dma("pool", bvH, b_in[1:2, :], [], [("E", "bvH")], ("m", 0))
    ada_load(0)
    ada_load(1)
    wi3 = w_in.rearrange("(k p) n -> p k n", p=128)
    slot_v, wv_v = wload(wi3[:, :, 1024:2048], KC, 1024, extra_reads=[("D", "ada", 1)])
    VEC = M.at("VEC", [NVEC, D], F32, OFF_S)
    CC = M.at("CC", [17, D], F32, OFF_S + 4096)
    WS32 = M.at("WS32", [128, 4, 128], F32, OFF_B)
    WTf = M.at("WTf", [128, 4, 128], F32, OFF_S + 14336)
    A8 = M.at("A8", [8, 4, 8], F32, OFF_B + 2048)
    Rm = M.at("Rm", [8, 128], F32, OFF_S + 16384 + 128)
    O1 = M.at("O1", [8, 4, 128], F32, OFF_S + 16384 + 640)
    R.group_keys.add(("setup",))
    R.group_keys.add(("setup2",))
    vec_srcs = [(R_BADA, 6, b_ada), (R_BIN, 6, b_in), (R_G1, 1, g_norm1), (R_LVG, 1, ln_v_g), (R_LVB, 1, ln_v_b),
                (R_BPA, 1, b_pa), (R_CB, 1, conv_b), (R_LCG, 1, ln_c_g), (R_LCB, 1, ln_c_b), (R_BPB, 1, b_pb),
                (R_G2, 1, g_norm2), (R_CW, 31, conv_w)]
    dma("sp", CC, cc, [], [("S", "CC")], ("setup",))
    for r0, nr, src in vec_srcs:
        dma("sp", VEC[r0:r0 + nr, :], src, [], [("S", "VEC", r0)], ("setup",))
    vec_keys = [("S", "VEC", r0) for r0, _, _ in vec_srcs]
    dma("sp", WS32, w_s.rearrange("g t s -> t g s"), [], [("B", 0)], ("setup2",))
    dma("sp", A8, w_s[:, 0:8, 0:8].rearrange("g b a -> b g a"), [], [("B", 1)], ("setup2",))

    pv = PS[3][:, 0:KC * NVEC].rearrange("p (c r) -> p c r", c=KC)
    for c in range(KC):
        tp(pv[:, c, :], VEC[:, c * 128:(c + 1) * 128], NVEC, vec_keys, [("ps", 6)])
    cp("dve", vecT, pv, [("ps", 6)], [("vecT",)])

    act(CC, CC, AF.Silu, [("S", "CC")], [("S", "CC")])
    pc = PS[3][:, 512:512 + KC * 17].rearrange("p (c r) -> p c r", c=KC)
    for c in range(KC):
        tp(pc[:, c, :], CC[:, c * 128:(c + 1) * 128], 17, [("S", "CC")], [("ps", 7)])
    cp("dve", cT, pc, [("ps", 7)], [("cT",)])
    ada_compute(0)
    ada_compute(1)

    dump("vecT", vecT, [128, KC, NVEC], [("vecT",)])
    dump("WT", WT, [128, 4, 128], [("WT",)], ) if False else None

    def modv(mi, tt):
        if tt < 16:
            return modT[:, mi * 8:(mi + 1) * 8, 16:17].to_broadcast([128, KC, 128])
        return modT[:, mi * 8:(mi + 1) * 8, 0:16].unsqueeze(3).to_broadcast([128, KC, 16, 8])

    def tokv(ap3, tt):
        if tt < 16:
            return ap3
        return ap3.rearrange("p k (q b) -> p k q b", b=8)

    def interleave(ga, na, gb, nb):
        ia = ib = 0
        ea = eb = False
        while not (ea and eb):
            if not ea and (eb or ia * nb <= ib * na):
                try:
                    next(ga)
                    ia += 1
                except StopIteration:
                    ea = True
            else:
                try:
                    next(gb)
                    ib += 1
                except StopIteration:
                    eb = True

    def drain(g):
        for _ in g:
            pass

    ncnt = {"n": 0}

    def newcol():
        ncnt["n"] += 1
        return ncnt["n"] % 64

    def rstd_from(col, src_ap, src_keys, scale):
        act(rstd[:, col:col + 1], src_ap, AF.Sqrt, list(src_keys) + [("epsT",)], [("rstd", col)], bias=epsT[:, 0:1], scale=scale)
        R.op("dve", lambda e: e.reciprocal(out=rstd[:, col:col + 1], in_=rstd[:, col:col + 1]), reads=[("rstd", col)], writes=[("rstd", col)])

    def rmsnorm_stages(src, src_keys, xn, kxn, sqj, pr, tt, dst_tile, dst_keys, mg, msh, xnb_split=False):
        col = newcol()
        pt = PS[pr].rearrange("p (c t) -> p c t", c=KC)
        pk = [("ps", 2 * pr), ("ps", 2 * pr + 1)]

        def sa():
            act(sqj, src, AF.Square, src_keys, [("S", "sqj"), ("ssum", col)], accum_out=ssum[:, col:col + 1])
            rstd_from(col, ssum[:, col:col + 1], [("ssum", col)], 1.0 / D)

        def sb():
            act(xn, src, AF.Identity, list(src_keys) + [("rstd", col)], [kxn], scale=rstd[:, col:col + 1])
            for c in range(KC):
                tp(pt[:, c, :], xn[:, c * 128:(c + 1) * 128], 128, [kxn], [("ps", 2 * pr + c // 4)])

        def sc():
            if xnb_split:
                for h in range(2):
                    cs = slice(4 * h, 4 * h + 4)
                    tt_("dve", tokv(pt[:, cs, :], tt), tokv(pt[:, cs, :], tt), modv(mg, tt)[:, cs], ALU.mult, [pk[h], ("modT", mg)], [pk[h]])
                for h in range(2):
                    cs = slice(4 * h, 4 * h + 4)
                    tt_("dve", tokv(dst_tile[:, cs, :], tt), tokv(pt[:, cs, :], tt), modv(msh, tt)[:, cs], ALU.add, [pk[h], ("modT", msh)], dst_keys)
                return
            tt_("dve", tokv(pt, tt), tokv(pt, tt), modv(mg, tt), ALU.mult, pk + [("modT", mg)], pk)
            tt_("dve", tokv(dst_tile, tt), tokv(pt, tt), modv(msh, tt), ALU.add, pk + [("modT", msh)], dst_keys)
        return sa, sb, sc

    def skewed(stage_lists, pre=None, reverse=False, post=None):
        n = len(stage_lists[0])
        nt = len(stage_lists)
        for step in range(nt + n - 1):
            for si in (range(n - 1, -1, -1) if reverse else range(n)):
                t = step - si
                if 0 <= t < nt:
                    if si == 0 and pre is not None:
                        pre(t)
                    stage_lists[t][si]()
                    if post is not None and si == post[0]:
                        post[1](t)
            yield

    OUTK = []

    def hkey(k, tt):
        return ("A", tt)

    brot = {"n": 0}

    def proj_fm_gen(slot, wv, kch, in_buf, in_key, njc, evac, sts=STMIX, banks=None):
        for j in range(njc):
            for (t0, w) in sts:
                if banks is None:
                    b = next_bank()
                else:
                    b = banks[brot["n"] % len(banks)]
                    brot["n"] += 1
                pb = bank(b)[:, 0:w]
                for k in range(kch):
                    rk = [("W", slot, 0), ("W", slot, 1)] + [in_key(k, tt) for tt in tts(t0, w)]
                    mm(pb, wv[:, k, j * 128:(j + 1) * 128], in_buf[:, k, t0:t0 + w], k == 0, k == kch - 1, rk, [("ps", b)])
                evac(j, t0, w, pb, ("ps", b))
                yield

    def proj_fm(w3, kch, in_buf, in_key, njc, evac, sts=STMIX):
        slot, wv = wload(w3, kch, njc * 128)
        drain(proj_fm_gen(slot, wv, kch, in_buf, in_key, njc, evac, sts))

    LVGs = M.at("LVGs", [128, D], F32, OFF_D + 32768)
    LVBs = M.at("LVBs", [128, D], F32, OFF_D + 36864)
    bvL = M.at("bvL", [1, D], BF16, OFF_B + 32768)
    bv32 = M.at("bv32", [1, D], F32, OFF_B + 28672)
    dma("sp", bv32, b_in[1:2, :], [], [("E", "bv32")], ("m", 1))
    dma("sp", LVGs, ln_v_g.partition_broadcast(128), [], [("E", "LVG")], ("m", 2))
    dma("sp", LVBs, ln_v_b.partition_broadcast(128), [], [("E", "LVB")], ("m", 3))
    tt_("dve", bvL, bv32, bvH, ALU.subtract, [("E", "bv32"), ("E", "bvH")], [("E", "bvL")])
    dma("sp", BV2[1:2, :], bvL, [("E", "bvL")], [("E", "bvL2")], ("m", 4))

    R.fence("S")
    NXIN = 8
    XIN = [M.at(f"xin{i}", [128, D], F32, OFF_C + i * 4096) for i in range(NXIN)]
    XN = [M.at(f"xn{i}", [128, D], F32, OFF_S + 12288 + i * 4096) for i in range(2)]
    SQJ = M.at("sqj", [128, D], BF16, OFF_S + 20480)
    n1 = []
    for tt in range(NTT):
        s = tt % NXIN
        n1.append(rmsnorm_stages(XIN[s], [("C", "xin", s)], XN[tt % 2], ("S", "xn", tt % 2), SQJ, 2 + tt % 2, tt,
                                 hT[:, :, tt * 128:(tt + 1) * 128], [("A", tt)], M_G1, M_SH1))

    def pre_x(tt):
        s = tt % NXIN
        dma("sp", XIN[s], xc[tt * 128:(tt + 1) * 128, :], [("D", "ada", 1)] if 3 <= tt < NXIN else [], [("C", "xin", s)], ("xin", s))
    for tt in range(NXIN):
        pre_x(tt)

    def pre_x2(tt):
        if tt + NXIN < NTT and tt >= 0:
            pre_x(tt + NXIN)
    drain(skewed(n1, post=(1, pre_x2)))
    R.fence("C")
    slot_u, wv_u = wload(wi3[:, :, 0:1024], KC, 1024, extra_reads=[("A", 2)])
    ada_load(2, extra_reads=[("A", 13)])
    ada_load(3, extra_reads=[("A", 13)])
    dump("hT", hT, [128, KC, T], [("A", tt) for tt in range(NTT)])

    R.fence("S")
    VR = [M.at(f"vr{i}", [128, D], F32, OFF_S + i * 4096) for i in range(3)]
    LVG = PS[2]
    LVB = PS[3]
    cp("dve", LVG, LVGs, [("E", "LVG")], [("ps", 4), ("ps", 5)])
    cp("dve", LVB, LVBs, [("E", "LVB")], [("ps", 6), ("ps", 7)])

    def v_stages(tt):
        slot, wv = slot_v, wv_v
        vr = VR[tt % 3]
        kv = ("S", "vr", tt % 3)
        col = newcol()

        def sa():
            p = next_pair()
            pp = PS[p]
            pk = [("ps", 2 * p), ("ps", 2 * p + 1)]
            for h in range(2):
                o_ = pp[:, h * 512:(h + 1) * 512]
                for k in range(KC):
                    mm(o_, hT[:, k, tt * 128:(tt + 1) * 128], wv[:, k, h * 512:(h + 1) * 512], k == 0, False,
                       [("A", tt), ("W", slot, 0), ("W", slot, 1)], [pk[h]])
                mm(o_, onesb[0:2, :], BV2[0:2, h * 512:(h + 1) * 512], False, True, [("onesb",), ("E", "bvH"), ("E", "bvL2")], [pk[h]])
            act(vr, pp, AF.Gelu_apprx_tanh, pk, [kv])

        def sb():
            for h in range(2):
                R.op("dve", (lambda o2, i2: (lambda e: e.bn_stats(out=o2, in_=i2)))(st6[:, tt % 2, h, :], vr[:, h * 512:(h + 1) * 512]),
                     reads=[kv], writes=[("st6", tt % 2, h)])
            R.op("dve", (lambda o2, i2: (lambda e: e.bn_aggr(out=o2, in_=i2)))(mv[:, tt % 2, :], st6[:, tt % 2, :, :].rearrange("p h s -> p (h s)")),
                 reads=[("st6", tt % 2, 0), ("st6", tt % 2, 1)], writes=[("mv", tt % 2)])
            act(rstd[:, col:col + 1], mv[:, tt % 2, 1:2], AF.Sqrt, [("mv", tt % 2), ("epsT",)], [("rstd", col)], bias=epsT[:, 0:1], scale=1.0)

        def sc():
            R.op("dve", lambda e: e.reciprocal(out=rstd[:, col:col + 1], in_=rstd[:, col:col + 1]), reads=[("rstd", col)], writes=[("rstd", col)])
            ts_("dve", vr, vr, mv[:, tt % 2, 0:1], rstd[:, col:col + 1], ALU.subtract, ALU.mult, [kv, ("mv", tt % 2), ("rstd", col)], [kv])
            tt_("dve", vr, vr, LVG, ALU.mult, [kv, ("ps", 4), ("ps", 5)], [kv])
            if tt < 15:
                tt_("dve", nB[:, tt, :], vr, LVB, ALU.add, [kv, ("ps", 6), ("ps", 7)], [("B", tt)])
            else:
                tt_("dve", vr, vr, LVB, ALU.add, [kv, ("ps", 6), ("ps", 7)], [kv])
                cp("dve", nB[:, tt, :], vr, [kv], [("B", tt)])
                dma("sp", cvp_out if tt == 15 else cvs_out, vr, [kv], [("out", "cv", tt)], ("cvout", tt))
                OUTK.append(("out", "cv", tt))
        return sa, sb, sc

    def gen_v():
        return skewed([v_stages(tt) for tt in range(NTT)])

    def evac_u(j, t0, w, pb, bk):
        act(U[:, j, t0:t0 + w], pb, AF.Gelu_apprx_tanh, [bk, ("vecT",)], [("C", j, tt) for tt in tts(t0, w)],
            bias=vecT[:, j, R_BIN + 0:R_BIN + 1])
    bs32 = M.at("bs32", [1, 2, 512], F32, OFF_D + 32768)
    BS2 = M.at("BS2", [2, 2, 512], BF16, OFF_D + 36864)
    bsH = BS2[0:1, :, :]
    bsL = M.at("bsL", [1, 2, 512], BF16, OFF_D + 38912)

    def wt_stage1():
        R.op("pool", lambda e: e.memset(Rm, 1.0), writes=[("S", "Rm")])
        R.op("pool", lambda e: e.affine_select(out=Rm, in_=Rm, pattern=[[0, 16], [1, 8]], compare_op=ALU.is_equal, fill=0.0,
                                               base=0, channel_multiplier=-1), reads=[("S", "Rm")], writes=[("S", "Rm")])
        b1 = next_bank()
        pw = bank(b1).rearrange("p (g t) -> p g t", g=4)
        for g in range(4):
            tp(pw[:, g, :], WS32[:, g, :], 128, [("B", 0)], [("ps", b1)])
        cp("dve", WTf, pw, [("ps", b1)], [("S", "WTf")])
        R.op("pool", lambda e: e.affine_select(out=WT, in_=WTf, pattern=[[0, 4], [1, 128]], compare_op=ALU.is_ge, fill=0.0,
                                               base=0, channel_multiplier=-1), reads=[("S", "WTf")], writes=[("WT",)])
        b2 = next_bank()
        po1 = bank(b2)[0:8, :].rearrange("p (g t) -> p g t", g=4)
        for g in range(4):
            mm(po1[:, g, :], A8[:, g, :], Rm, True, True, [("B", 1), ("S", "Rm")], [("ps", b2)])
        cp("dve", O1, po1, [("ps", b2)], [("S", "O1")])
        dma("sp", bs32[:, 0, :], b_s, [], [("S", "bs32"), ("E", "LVG")], ("m", 1))

    def wt_stage2():
        b1 = next_bank()
        pw = bank(b1).rearrange("p (g t) -> p g t", g=4)
        for g in range(4):
            mm(pw[:, g, :], Rm, O1[:, g, :], True, True, [("S", "O1"), ("S", "Rm")], [("ps", b1)])
        cp("dve", WTf, pw, [("ps", b1)], [("S", "WTf")])
        R.op("pool", lambda e: e.affine_select(out=WTf, in_=WTf, pattern=[[0, 4], [8, 16], [1, 8]], compare_op=ALU.is_ge, fill=0.0,
                                               base=0, channel_multiplier=-1), reads=[("S", "WTf")], writes=[("S", "WTf")])
        R.op("pool", lambda e: e.affine_select(out=WTS, in_=WTf, pattern=[[0, 4], [-8, 16], [0, 8]], compare_op=ALU.is_ge, fill=0.0,
                                               base=0, channel_multiplier=1), reads=[("S", "WTf")], writes=[("WTS",)])
        cp("dve", bs32[:, 1, :].rearrange("p (g q b) -> p g q b", g=4, q=16),
           bs32[:, 0, :].rearrange("p (g t) -> p g t", g=4)[:, :, 0:8].unsqueeze(2).to_broadcast([1, 4, 16, 8]),
           [("S", "bs32")], [("S", "bs32b")])
        cp("dve", bsH, bs32, [("S", "bs32"), ("S", "bs32b")], [("S", "bsH"), ("E", "LVB")])
        tt_("dve", bsL, bs32, bsH, ALU.subtract, [("S", "bs32"), ("S", "bs32b"), ("S", "bsH")], [("S", "bsL"), ("E", "LVB")])
        dma("sp", BS2[1:2, :, :], bsL, [("S", "bsL")], [("S", "bsL2")], ("m", 4))

    wt_stage1()
    vgen = gen_v()
    ugen = proj_fm_gen(slot_u, wv_u, KC, hT, hkey, 8, evac_u)
    next(vgen)
    next(ugen)
    next(vgen)
    next(ugen)
    wt_stage2()
    for _ in range(4):
        next(vgen)
        next(ugen)
    ada_compute(2, rot=True)
    ada_compute(3, rot=True)
    wv_ga = ADA[0]
    dma("pool", wv_ga, wi3[:, :, 4096:5120], [], [("D", "ada", 0)], ("ada", 0))
    ada_load(4, buf=1)
    interleave(vgen, NTT - 4, ugen, 20)
    dump("n", nB, [128, NTT, D], [("B", tt) for tt in range(NTT)])
    dump("U", U, [128, KC, T], [("C", j, tt) for j in range(KC) for tt in range(NTT)])

    R.fence("S")

    def gen_mix():
        for tt in range(NTT):
            gi = 0 if tt < 16 else 1
            Wm, wk = (WT, ("WT",)) if tt < 16 else (WTS, ("WTS",))
            for jg in range(2):
                b = next_bank()
                pb = bank(b).rearrange("p (j t) -> p j t", j=4)
                for jj in range(4):
                    j = jg * 4 + jj
                    g = j // 2
                    mm(pb[:, jj, :], nB[:, tt, j * 128:(j + 1) * 128], Wm[:, g, :], True, False, [("B", tt), wk], [("ps", b)])
                    mm(pb[:, jj, :], onesb[0:2, :], BS2[0:2, gi, g * 128:(g + 1) * 128], False, True, [("onesb",), ("S", "bsH"), ("S", "bsL2")], [("ps", b)])
                uk = [("C", j, tt) for j in range(jg * 4, jg * 4 + 4)]
                uv = U[:, jg * 4:(jg + 1) * 4, tt * 128:(tt + 1) * 128]
                tt_("dve", uv, pb, uv, ALU.mult, [("ps", b)] + uk, uk)
            yield
    ada_compute(4, buf=1)
    ada_load(5, buf=1)
    ga_jobs = [(j, t0, w) for j in range(KC) for (t0, w) in STMIX]

    def ga_alias(j, t0, w):
        return (j * 4352 + 2 * t0) // 2048, (j * 4352 + 2 * (t0 + w) - 1) // 2048

    def emit_ga(j, t0, w):
        b = next_bank()
        pb = bank(b)[:, 0:w]
        for k in range(KC):
            rk = [("D", "ada", 0)] + [("A", tt) for tt in tts(t0, w)]
            mm(pb, wv_ga[:, k, j * 128:(j + 1) * 128], hT[:, k, t0:t0 + w], k == 0, k == KC - 1, rk, [("ps", b)])
        lo, hi = ga_alias(j, t0, w)
        act(SA[:, j, t0:t0 + w], pb, AF.Sigmoid, [("ps", b), ("vecT",)],
            [("B", j, tt) for tt in tts(t0, w)] + [("B", t2) for t2 in range(lo, hi + 1)], bias=vecT[:, j, R_BIN + 4:R_BIN + 5])

    mixg = gen_mix()
    gi_ = 0
    for t in range(NTT):
        next(mixg)
        for _ in range(3):
            if gi_ < len(ga_jobs) and ga_alias(*ga_jobs[gi_])[1] <= t:
                emit_ga(*ga_jobs[gi_])
                gi_ += 1
    while gi_ < len(ga_jobs):
        emit_ga(*ga_jobs[gi_])
        gi_ += 1
    dump("ug", U, [128, KC, T], [("C", j, tt) for j in range(KC) for tt in range(NTT)])

    def evac_sig(dst, reg, brow):
        def f(j, t0, w, pb, bk):
            act(dst[:, j, t0:t0 + w], pb, AF.Sigmoid, [bk, ("vecT",)], [(reg, j, tt) for tt in tts(t0, w)],
                bias=vecT[:, j, brow:brow + 1])
        return f
    ada_compute(5, buf=1)
    dump("modT", modT, [128, 48, 17], [("modT", i) for i in range(6)])

    def evac_ma(j, t0, w, pb, bk):
        ks = [("B", j, tt) for tt in tts(t0, w)]
        stt(SA[:, j, t0:t0 + w], pb, vecT[:, j, R_BPA:R_BPA + 1], SA[:, j, t0:t0 + w], ALU.add, ALU.mult, [bk, ("vecT",)] + ks, ks)
    proj_fm(w_pa.rearrange("(k p) n -> p k n", p=128), KC, U, lambda k, tt: ("C", k, tt), 8, evac_ma)
    dump("MA", SA, [128, KC, T], [("B", j, tt) for j in range(KC) for tt in range(NTT)])

    R.fence("C")
    R.fence("S")
    R.fence("D")
    R.fence("E")
    SGB = U
    proj_fm(wi3[:, :, 3072:4096], KC, hT, hkey, 8, evac_sig(SGB, "C", R_BIN + 3))
    GP = M.at("GP", [128, KC, 2080], BF16, OFF_D)
    GS = M.at("GS", [128, KC, 16, 38], BF16, OFF_D + 33280)
    GF32 = M.at("GF32", [128, KC, 160], F32, OFF_S)
    CA = [M.at(f"CA{i}", [128, D], F32, OFF_S + 5120 + i * 4096) for i in range(2)]
    STG = [M.at(f"STG{i}", [128, D], F32, OFF_S + 13312 + i * 4096) for i in range(2)]
    R.op("dve", lambda e: e.memset(GP[:, :, 0:30], 0.0), writes=[("D", "hist")])
    for gq in range(4):
        s_ = gq % 2
        dma("sp", CA[s_][0:120, :], cache[gq * 4:(gq + 1) * 4].rearrange("q r d -> (q r) d"), [], [("S", "ca", s_)], ("ca", s_))
        pr = 2 + (gq % 2)
        pt = PS[pr].rearrange("p (c t) -> p c t", c=KC)
        for c in range(KC):
            tp(pt[:, c, 0:120], CA[s_][0:120, c * 128:(c + 1) * 128], 120, [("S", "ca", s_)], [("ps", 2 * pr + c // 4)])
        cp("dve", GS[:, :, gq * 4:(gq + 1) * 4, 0:30], pt[:, :, 0:120].rearrange("p c (q r) -> p c q r", q=4),
           [("ps", 2 * pr), ("ps", 2 * pr + 1)], [("D", "gsh", gq)])

    def evac_glu(j, t0, w, pb, bk):
        bga = vecT[:, j, R_BIN + 2:R_BIN + 3]
        ck = [("C", j, tt) for tt in tts(t0, w)]
        if t0 < 2048:
            stt(GP[:, j, 30 + t0:30 + t0 + w], pb, bga, SGB[:, j, t0:t0 + w], ALU.add, ALU.mult, [bk, ("vecT",)] + ck,
                [("D", j, tt) for tt in tts(t0, w)])
            if t0 == 1536:
                stt(GF32[:, j, 0:32], pb[:, 480:512], bga, SGB[:, j, 2016:2048], ALU.add, ALU.mult, [bk, ("vecT",)] + ck, [("S", "gf", j, 0)])
        else:
            stt(GS[:, j, :, 30:38], pb.rearrange("p (q b) -> p q b", b=8), bga, SGB[:, j, 2048:2176].rearrange("p (q b) -> p q b", b=8),
                ALU.add, ALU.mult, [bk, ("vecT",)] + ck, [("D", j, 16)])
            stt(GF32[:, j, 32:160], pb, bga, SGB[:, j, 2048:2176], ALU.add, ALU.mult, [bk, ("vecT",)] + ck, [("S", "gf", j, 1)])
    proj_fm(wi3[:, :, 2048:3072], KC, hT, hkey, 8, evac_glu, sts=ST512)
    pt = PS[2].rearrange("p (c t) -> p c t", c=KC)
    for j in range(KC):
        tp(pt[0:30, j, :], GF32[:, j, 2:32], 128, [("S", "gf", j, 0)], [("ps", 4 + j // 4)])
    cp("dve", STG[0][0:30, :], PS[2][0:30, :], [("ps", 4), ("ps", 5)], [("S", "stg", 0)])
    dma("sp", ncp_out, STG[0][0:30, :], [("S", "stg", 0)], [("out", "ncp")], ("ncp",))
    OUTK.append(("out", "ncp"))
    pt = PS[3].rearrange("p (c t) -> p c t", c=KC)
    for j in range(KC):
        tp(pt[:, j, :], GF32[:, j, 32:160], 128, [("S", "gf", j, 1)], [("ps", 6 + j // 4)])
    cp("dve", STG[1], PS[3], [("ps", 6), ("ps", 7)], [("S", "stg", 1)])
    R.group_keys.add(("ncs",))
    for q in range(16):
        dma("sp", ncs_out[q, 22:30, :], STG[1][q * 8:(q + 1) * 8, :], [("S", "stg", 1)], [("out", "ncs", q)], ("ncs",))
        OUTK.append(("out", "ncs", q))
    dma("sp", ncs_out[:, 0:22, :], cache[:, 8:30, :], [], [("out", "ncs0")], ("ncs",))
    OUTK.append(("out", "ncs0"))
    dump("GP", GP[:, :, 0:2078], [128, KC, 2078], [("D", "hist")] + [("D", j, tt) for j in range(KC) for tt in range(16)])
    dump("GS", GS, [128, KC, 16, 38], [("D", "gsh", g) for g in range(4)] + [("D", j, 16) for j in range(KC)])

    R.fence("C")
    SB = U
    proj_fm(wi3[:, :, 5120:6144], KC, hT, hkey, 8, evac_sig(SB, "C", R_BIN + 5))

    R.fence("S")
    R.fence("A")
    CV = M.at("CV", [128, KC, T], BF16, OFF_A)
    DG = [M.at(f"DG{i}", [128, 31, 128], BF16, OFF_S + i * 7936) for i in range(2)]
    slot_pb, wv_pb = wload(w_pb.rearrange("(k p) n -> p k n", p=128), KC, 1024)
    NPE = 26

    def build_dg(c):
        tt_("dve", DG[c % 2], identb.unsqueeze(1).to_broadcast([128, 31, 128]),
            vecT[:, c, R_CW:R_CW + 31].unsqueeze(2).to_broadcast([128, 31, 128]), ALU.mult, [("identb",), ("vecT",)], [("S", "dg", c % 2)])
    build_dg(0)
    build_dg(1)
    for c in range(KC):
        dg = DG[c % 2]
        kd = ("S", "dg", c % 2)
        for (t0, w) in ST512:
            b = next_bank()
            pb = bank(b)[:, 0:w]
            if t0 < 2048:
                rk = [kd] + [("D", c, tt) for tt in range(max(0, (t0 - 30) // 128), (t0 + w - 1) // 128 + 1)]
                if t0 == 0:
                    rk.append(("D", "hist"))
                for k in range(NPE):
                    mm(pb, dg[:, k, :], GP[:, c, t0 + k:t0 + k + w], k == 0, k == NPE - 1, rk, [("ps", b)])
                for k in range(NPE, 31):
                    stt(pb, GP[:, c, t0 + k:t0 + k + w], vecT[:, c, R_CW + k:R_CW + k + 1], pb, ALU.mult, ALU.add,
                        rk[1:] + [("ps", b), ("vecT",)], [("ps", b)])
            else:
                rk = [kd, ("D", c, 16)] + [("D", "gsh", g) for g in range(4)]
                for k in range(NPE):
                    mm(pb, dg[:, k, :], GS[:, c, :, k:k + 8], k == 0, k == NPE - 1, rk, [("ps", b)])
                pb3 = pb.rearrange("p (q b) -> p q b", b=8)
                for k in range(NPE, 31):
                    stt(pb3, GS[:, c, :, k:k + 8], vecT[:, c, R_CW + k:R_CW + k + 1], pb3, ALU.mult, ALU.add,
                        rk[1:] + [("ps", b), ("vecT",)], [("ps", b)])
            act(CV[:, c, t0:t0 + w], pb, AF.Identity, [("ps", b), ("vecT",)], [("A", c, tt) for tt in tts(t0, w)],
                bias=vecT[:, c, R_CB:R_CB + 1])
        if c + 2 < KC:
            build_dg(c + 2)
    dump("CV", CV, [128, KC, T], [("A", j, tt) for j in range(KC) for tt in range(NTT)])

    R.fence("S")
    NF = [M.at(f"NF{i}", [128, D], F32, OFF_S + i * 4096) for i in range(2)]
    VEPS = M.at("veps", [128, 32], F32, OFF_S + 8192)

    def ln_stages(tt):
        pa = PS[2]
        pa3 = pa.rearrange("p (c t) -> p c t", c=KC)
        ka = [("ps", 4), ("ps", 5)]
        pcp = 3 if tt % 2 == 0 else 1
        pc = PS[pcp].rearrange("p (c t) -> p c t", c=KC)
        kc_ = [("ps", 2 * pcp), ("ps", 2 * pcp + 1)]
        nf = NF[tt % 2]
        knf = ("S", "nf", tt % 2)
        col = newcol()

        def sa():
            for c in range(KC):
                mm(pa3[:, c, :], CV[:, c, tt * 128:(tt + 1) * 128], identb, True, True, [("A", c, tt), ("identb",)], [ka[c // 4]])
            act(nf, pa, AF.Identity, ka, [knf])

        def sb():
            for h in range(2):
                R.op("dve", (lambda o2, i2: (lambda e: e.bn_stats(out=o2, in_=i2)))(st6[:, tt % 2, h, :], nf[:, h * 512:(h + 1) * 512]),
                     reads=[knf], writes=[("st6", tt % 2, h)])
            R.op("dve", (lambda o2, i2: (lambda e: e.bn_aggr(out=o2, in_=i2)))(mv[:, tt % 2, :], st6[:, tt % 2, :, :].rearrange("p h s -> p (h s)")),
                 reads=[("st6", tt % 2, 0), ("st6", tt % 2, 1)], writes=[("mv", tt % 2)])
            ts_("dve", VEPS[:, tt:tt + 1], mv[:, tt % 2, 1:2], EPS, None, ALU.add, None, [("mv", tt % 2)], [("S", "veps", tt)])
            tt_("pool", rstd[:, col:col + 1], VEPS[:, tt:tt + 1], negh, ALU.pow, [("S", "veps", tt), ("negh",)], [("rstd", col)])

        def sc():
            ts_("dve", nf, nf, mv[:, tt % 2, 0:1], rstd[:, col:col + 1], ALU.subtract, ALU.mult, [knf, ("mv", tt % 2), ("rstd", col)], [knf])
            for c in range(KC):
                tp(pc[:, c, :], nf[:, c * 128:(c + 1) * 128], 128, [knf], [kc_[c // 4]])

        def sd():
            for c in range(KC):
                act(CV[:, c, tt * 128:(tt + 1) * 128], pc[:, c, :], AF.Silu, [kc_[c // 4], ("vecT",)], [("A", c, tt)],
                    bias=vecT[:, c, R_LCB:R_LCB + 1], scale=vecT[:, c, R_LCG:R_LCG + 1])
        return sa, sb, sc, sd

    def evac_m(j, t0, w, pb, bk):
        kb = [("B", j, tt) for tt in tts(t0, w)]
        kc = [("C", j, tt) for tt in tts(t0, w)]
        stt(pb, pb, vecT[:, j, R_BPB:R_BPB + 1], SB[:, j, t0:t0 + w], ALU.add, ALU.mult, [bk, ("vecT",)] + kc, [bk])
        tt_("dve", SA[:, j, t0:t0 + w], pb, SA[:, j, t0:t0 + w], ALU.add, [bk] + kb, kb)

    ckey = lambda k, tt: ("A", k, tt)
    lng = skewed([ln_stages(tt) for tt in range(NTT)], reverse=True)
    pend = []
    done_st = 0
    jb = {"n": 0}
    for step in range(NTT + 3 + 64):
        cur = []
        for _ in range(2):
            if pend:
                j, t0, w = pend.pop(0)
                b = jb["n"] % 2
                jb["n"] += 1
                pb = bank(b)[:, 0:w]
                for k in range(KC):
                    rk = [("W", slot_pb, 0), ("W", slot_pb, 1)] + [("A", k, tt) for tt in tts(t0, w)]
                    mm(pb, wv_pb[:, k, j * 128:(j + 1) * 128], CV[:, k, t0:t0 + w], k == 0, k == KC - 1, rk, [("ps", b)])
                cur.append((j, t0, w, pb, ("ps", b)))
        alive = next(lng, "end") != "end"
        ndone = step - 3 + 1
        while done_st < len(STMIX) and ndone * 128 >= STMIX[done_st][0] + STMIX[done_st][1]:
            pend.extend((j, STMIX[done_st][0], STMIX[done_st][1]) for j in range(KC))
            done_st += 1
        for (j, t0, w, pb, bk) in cur:
            evac_m(j, t0, w, pb, bk)
        if not alive and not pend and done_st == len(STMIX):
            break
    dump("S", CV, [128, KC, T], [("A", j, tt) for j in range(KC) for tt in range(NTT)])
    dump("M", SA, [128, KC, T], [("B", j, tt) for j in range(KC) for tt in range(NTT)])

    def build_gt(mi, dst, dkey, gx, kgx):
        for gi in range(2):
            ttx = 0 if gi == 0 else 16
            cp("dve", tokv(gx, ttx), modv(mi, ttx), [("modT", mi)], [kgx])
            pr = 2 + gi
            pt_ = PS[pr].rearrange("p (c t) -> p c t", c=KC)
            for c in range(KC):
                tp(pt_[:, c, :], gx[:, c, :], 128, [kgx], [("ps", 2 * pr + c // 4)])
            cp("dve", dst[:, gi, :], PS[pr], [("ps", 2 * pr), ("ps", 2 * pr + 1)], [(dkey[0], dkey[1], gi)])

    R.fence("S")
    R.fence("D")
    XIN2 = [M.at(f"xin2_{i}", [128, D], F32, OFF_S + i * 4096) for i in range(3)]
    XNB = [M.at(f"xnb{i}", [128, D], F32, OFF_S + 12288 + i * 4096) for i in range(2)]
    SQJ2 = M.at("sqj2", [128, D], BF16, OFF_S + 20480)
    GXv = XNB[0].rearrange("p (c t) -> p c t", c=KC)
    GT1 = M.at("GT1", [128, 2, D], F32, OFF_D)
    h2T = M.at("h2T", [128, KC, 1152], BF16, OFF_B + 59392)
    ACTB = M.at("ACTB", [128, FC, 1152], BF16, OFF_B)
    GT2 = M.at("GT2", [128, 2, D], F32, OFF_B + 50688)
    build_gt(M_GT1, GT1, ("D", "gt1"), GXv, ("S", "xn", 0))
    R.fence_into("X", ["A", "C"])
    X1 = M.at("X1", [128, NTT, D], F32, OFF_A)
    GROUPS = (list(range(0, 9)), list(range(9, 17)))

    def norm2_stages(i, tt):
        return rmsnorm_stages(X1[:, tt, :], [("X", tt, q) for q in range(4)], XNB[i % 2], ("S", "xn", i % 2), SQJ2, 2 + i % 2, tt,
                              h2T[:, :, i * 128:(i + 1) * 128], [("D", "h", i)], M_G2, M_SH2, xnb_split=True)

    slot, wv = wload(w_o.rearrange("(k p) n -> p k n", p=128), KC, 1024)
    for tt in range(NTT):
        s_ = tt % 3
        dma("sp", XIN2[s_], xc[tt * 128:(tt + 1) * 128, :], [], [("S", "xin", s_)], ("xin", s_))
        p = next_pair()
        pp = PS[p]
        pk = [("ps", 2 * p), ("ps", 2 * p + 1)]
        for h in range(2):
            for k in range(KC):
                mm(pp[:, h * 512:(h + 1) * 512], SA[:, k, tt * 128:(tt + 1) * 128], wv[:, k, h * 512:(h + 1) * 512], k == 0, k == KC - 1,
                   [("B", k, tt), ("W", slot, 0), ("W", slot, 1)], [pk[h]])
        gi = 0 if tt < 16 else 1
        for h in range(2):
            hs = slice(h * 512, (h + 1) * 512)
            tt_("dve", pp[:, hs], pp[:, hs], GT1[:, gi, hs], ALU.mult, [pk[h], ("D", "gt1", gi)], [pk[h]])
        for h in range(2):
            hs = slice(h * 512, (h + 1) * 512)
            tt_("dve", X1[:, tt, hs], pp[:, hs], XIN2[s_][:, hs], ALU.add, [pk[h], ("S", "xin", s_)], [("X", tt, 2 * h), ("X", tt, 2 * h + 1)])
        if tt == 1:
            g0gen = skewed([norm2_stages(i, t_) for i, t_ in enumerate(GROUPS[0])])
        nst_ = len(GROUPS[0]) + 2
        if tt >= 1 and (tt * nst_) // 16 > ((tt - 1) * nst_) // 16:
            next(g0gen, None)
    drain(g0gen)
    dump("X1", X1, [128, NTT, D], [("X", tt, q) for tt in range(NTT) for q in range(4)])

    R.fence_into("F", ["B", "D"])
    R.fence("S")
    SG = [M.at(f"SG{i}", [128, 512], F32, OFF_S + i * 2048) for i in range(2)]
    GFB = M.at("GFB", [128, D], F32, OFF_S + 4096)
    dma("sp", GFB, g_final.partition_broadcast(128), [], [("S", "gfb")], ("m", 2))
    build_gt(M_GT2, GT2, ("F", "gt2"), GXv, ("S", "xn", 0))
    wf3 = w_ffn_in.rearrange("(k p) n -> p k n", p=128)
    wo3 = w_ffn_out.rearrange("(j p) n -> p j n", p=128)
    R.group_keys.add(("yout",))
    sgc = {"n": 0}

    gfb = {"ap": GFB, "keys": [("S", "gfb")]}

    def final_tile(tt):
        col = newcol()
        xk = [("X", tt, q) for q in range(4)]
        act(SQJ2, X1[:, tt, :], AF.Square, xk, [("S", "sqj"), ("ssum", col)], accum_out=ssum[:, col:col + 1])
        act(rstd[:, col:col + 1], ssum[:, col:col + 1], AF.Sqrt, [("ssum", col), ("epsT",)], [("rstd", col)], bias=epsT[:, 0:1], scale=1.0 / D)

        def part_b():
            R.op("dve", lambda e: e.reciprocal(out=rstd[:, col:col + 1], in_=rstd[:, col:col + 1]), reads=[("rstd", col)], writes=[("rstd", col)])
            stt(X1[:, tt, :], X1[:, tt, :], rstd[:, col:col + 1], gfb["ap"], ALU.mult, ALU.mult, xk + [("rstd", col)] + gfb["keys"], xk)
            dma("sp", y_out[tt * 128:(tt + 1) * 128, :], X1[:, tt, :], xk, [("out", "y", tt)], ("yout",))
            OUTK.append(("out", "y", tt))
        return part_b

    def gen_ffn_in(tiles):
        nloc = len(tiles) * 128
        lst = [(0, 384), (384, 384), (768, 384)] if nloc == 1152 else [(l0, min(512, nloc - l0)) for l0 in range(0, nloc, 512)]
        for wl in range(6):
            j0 = wl * 4
            nj = min(4, FC - j0)
            slot = wstate["n"] % 2
            wstate["n"] += 1
            wv = wview(slot, KC, 1024)
            dma("pool", wv[:, :, 0:nj * 128], wf3[:, :, j0 * 128:(j0 + nj) * 128], [], [("W", slot, 0)], ("W", slot, 0))
            dma("pool", wv[:, :, 512:512 + nj * 128], wf3[:, :, DFF + j0 * 128:DFF + (j0 + nj) * 128], [], [("W", slot, 1)], ("W", slot, 1))
            for jj in range(nj):
                j = j0 + jj
                for (l0, w) in lst:
                    bg = next_bank()
                    bu = next_bank()
                    hk = [("D", "h", i) for i in tts(l0, w)]
                    for k in range(KC):
                        mm(bank(bg)[:, 0:w], wv[:, k, jj * 128:(jj + 1) * 128], h2T[:, k, l0:l0 + w], k == 0, k == KC - 1,
                           [("W", slot, 0)] + hk, [("ps", bg)])
                    for k in range(KC):
                        mm(bank(bu)[:, 0:w], wv[:, k, 512 + jj * 128:512 + (jj + 1) * 128], h2T[:, k, l0:l0 + w], k == 0, k == KC - 1,
                           [("W", slot, 1)] + hk, [("ps", bu)])
                    sg = SG[sgc["n"] % 2]
                    ksg = ("S", "sg", sgc["n"] % 2)
                    sgc["n"] += 1
                    act(sg[:, 0:w], bank(bg)[:, 0:w], AF.Silu, [("ps", bg)], [ksg])
                    tt_("dve", ACTB[:, j, l0:l0 + w], bank(bu)[:, 0:w], sg[:, 0:w], ALU.mult, [("ps", bu), ksg],
                        [("F", "a", j, i) for i in tts(l0, w)])
                    yield

    defer = [None]

    def gen_ffn_out(tiles):
        for q in range(4):
            slot, wv = wload(wo3[:, :, q * 256:(q + 1) * 256], FC, 256)
            for i, tt in enumerate(tiles):
                b = next_bank()
                pb = bank(b)[:, 0:256]
                for j in range(FC):
                    mm(pb, ACTB[:, j, i * 128:(i + 1) * 128], wv[:, j, :], j == 0, j == FC - 1,
                       [("F", "a", j, i), ("W", slot, 0), ("W", slot, 1)], [("ps", b)])
                gi = 0 if tt < 16 else 1
                xq = X1[:, tt, q * 256:(q + 1) * 256]
                tt_("dve", pb, pb, GT2[:, gi, q * 256:(q + 1) * 256], ALU.mult, [("ps", b), ("F", "gt2", gi)], [("ps", b)])
                tt_("dve", xq, pb, xq, ALU.add, [("ps", b), ("X", tt, q)], [("X", tt, q)])
                if q == 3:
                    pb_new = final_tile(tt)
                    if defer[0] is not None:
                        defer[0]()
                    defer[0] = pb_new
                yield
        if defer[0] is not None:
            defer[0]()
            defer[0] = None

    def gen_norm2(tiles):
        return skewed([norm2_stages(i, t_) for i, t_ in enumerate(tiles)])

    drain(gen_ffn_in(GROUPS[0]))
    interleave(gen_ffn_out(GROUPS[0]), 4 * len(GROUPS[0]), gen_norm2(GROUPS[1]), (len(GROUPS[1]) + 2) * 2)
    cp("dve", PS[3], GFB, [("S", "gfb")], [("ps", 6), ("ps", 7)])
    gfb["ap"] = PS[3]
    gfb["keys"] = [("ps", 6), ("ps", 7)]
    drain(gen_ffn_in(GROUPS[1]))
    drain(gen_ffn_out(GROUPS[1]))

    out_keys = [("dbgout", n) for n in dbg_outs] + OUTK
    R.op("sp", lambda e: e.nop(), reads=out_keys, writes=[])
    nops, nsem = R.emit()
    return nc, dbg_outs, (nops, nsem)


_CACHE = {}


def _prep_inputs(inputs, b):
    f = lambda a: np.ascontiguousarray(np.asarray(a, dtype=np.float32))
    xs = f(inputs["x_sample"])[16 * b:16 * b + 16].reshape(128, D)
    m = {
        "xc": np.concatenate([f(inputs["x_prompt"])[b], xs], axis=0),
        "cc": np.concatenate([f(inputs["c_sample"])[16 * b:16 * b + 16], f(inputs["c_prompt"])[b:b + 1]], axis=0),
        "cache": f(inputs["cache_conv"])[0, 16 * b:16 * b + 16],
        "w_ada": f(inputs["w_ada"])[0], "b_ada": f(inputs["b_ada"])[0].reshape(6, D),
        "g_norm1": f(inputs["g_norm1"]), "w_in": f(inputs["w_in"])[0], "b_in": f(inputs["b_in"])[0].reshape(6, D),
        "ln_v_g": f(inputs["ln_v_g"]), "ln_v_b": f(inputs["ln_v_b"]),
        "w_s": f(inputs["w_s"])[0], "b_s": f(inputs["b_s"])[0].reshape(1, 512),
        "w_pa": f(inputs["w_pa"])[0], "b_pa": f(inputs["b_pa"]),
        "conv_w": f(inputs["conv_w"])[0], "conv_b": f(inputs["conv_b"]),
        "ln_c_g": f(inputs["ln_c_g"]), "ln_c_b": f(inputs["ln_c_b"]),
        "w_pb": f(inputs["w_pb"])[0], "b_pb": f(inputs["b_pb"]),
        "w_o": f(inputs["w_o"])[0], "g_norm2": f(inputs["g_norm2"]),
        "w_ffn_in": f(inputs["w_ffn_in"])[0], "w_ffn_out": f(inputs["w_ffn_out"])[0],
        "g_final": f(inputs["g_final"]).reshape(1, D),
    }
    return {k: np.ascontiguousarray(v) for k, v in m.items()}


def kernel(**inputs):
    if "nc" not in _CACHE:
        _CACHE["nc"] = build()[0]
    nc = _CACHE["nc"]
    in_maps = [_prep_inputs(inputs, b) for b in range(NCORES)]
    res = run_bass_kernel_spmd(nc, in_maps, core_ids=list(range(NCORES)))
    rs = res.results
    y_prompt = np.stack([r["y"][:2048] for r in rs]).astype(np.float32)
    y_sample = np.concatenate([r["y"][2048:].reshape(16, 8, D) for r in rs]).astype(np.float32)
    ncp = np.stack([r["ncp"] for r in rs])[None].astype(np.float32)
    ncs = np.concatenate([r["ncs"] for r in rs])[None].astype(np.float32)
    cvp = np.stack([r["cvp"] for r in rs])[None].astype(np.float32)
    cvs = np.concatenate([r["cvs"].reshape(16, 8, D) for r in rs])[None].astype(np.float32)
    return (y_prompt, y_sample, ncp, ncs, cvp, cvs)
```

```python
import numpy as np
import concourse.bass as bass
import concourse.mybir as mybir
from concourse.bass_utils import run_bass_kernel_spmd

F32 = mybir.dt.float32
BF16 = mybir.dt.bfloat16
AF = mybir.ActivationFunctionType
ALU = mybir.AluOpType

D = 1024
KC = 8
T = 2176
NTT = 17
DFF = 2816
FC = 22
EPS = 1e-6
NCORES = 8
SB_BASE = 16512
SB_LIMIT = 229376

R_BADA, R_BIN, R_G1, R_LVG, R_LVB, R_BPA, R_CB, R_LCG, R_LCB, R_BPB, R_G2, R_CW = 0, 6, 12, 13, 14, 15, 16, 17, 18, 19, 20, 21
NVEC = 52
M_SH1, M_G1, M_GT1, M_SH2, M_G2, M_GT2 = 0, 1, 2, 3, 4, 5


class _Op:
    __slots__ = ("eng", "fn", "deps_eng", "deps_dma", "dma_key", "needs_inc", "inc_val", "sem_val")

    def __init__(self, eng, fn, deps_eng, deps_dma, dma_key):
        self.eng = eng
        self.fn = fn
        self.deps_eng = deps_eng
        self.deps_dma = deps_dma
        self.dma_key = dma_key
        self.needs_inc = False
        self.inc_val = 0
        self.sem_val = 0


class Rec:
    def __init__(self, nc):
        self.nc = nc
        self.ops = []
        self.state = {}
        self.fences = {}
        self.group_keys = set()

    def _add_dep(self, de, dd, idx):
        if idx is None:
            return
        o = self.ops[idx]
        if o.dma_key is not None:
            dd.add(idx)
        else:
            if de.get(o.eng, -1) < idx:
                de[o.eng] = idx

    def op(self, eng, fn, reads=(), writes=(), dma_key=None):
        idx = len(self.ops)
        de, dd = {}, set()
        for k in list(reads) + list(writes):
            if k not in self.state:
                f = self.fences.get(k[0])
                if f is not None:
                    for e, i in f[0].items():
                        if de.get(e, -1) < i:
                            de[e] = i
                    dd |= f[1]
        for k in reads:
            st = self.state.get(k)
            if st is not None:
                self._add_dep(de, dd, st[0])
        for k in writes:
            st = self.state.get(k)
            if st is not None:
                self._add_dep(de, dd, st[0])
                for e, i in st[1].items():
                    if de.get(e, -1) < i:
                        de[e] = i
                dd |= st[2]
        self.ops.append(_Op(eng, fn, de, dd, dma_key))
        is_dma = dma_key is not None
        for k in reads:
            st = self.state.setdefault(k, [None, {}, set()])
            if is_dma:
                st[2].add(idx)
            else:
                st[1][eng] = idx
        for k in writes:
            self.state[k] = [idx, {}, set()]
        return idx

    def fence_into(self, new_region, olds):
        de, dd = {}, set()
        for r in olds:
            self.fence(r)
            f = self.fences[r]
            for e, i in f[0].items():
                if de.get(e, -1) < i:
                    de[e] = i
            dd |= f[1]
        old = self.fences.get(new_region)
        if old is not None:
            for e, i in old[0].items():
                if de.get(e, -1) < i:
                    de[e] = i
            dd |= old[1]
        self.fence(new_region)
        f = self.fences[new_region]
        for e, i in f[0].items():
            if de.get(e, -1) < i:
                de[e] = i
        dd |= f[1]
        self.fences[new_region] = (de, dd)

    def fence(self, region):
        de, dd = {}, set()
        old = self.fences.get(region)
        if old is not None:
            de.update(old[0])
            dd |= old[1]
        for k in [k for k in self.state if k[0] == region]:
            st = self.state.pop(k)
            self._add_dep(de, dd, st[0])
            for e, i in st[1].items():
                if de.get(e, -1) < i:
                    de[e] = i
            dd |= st[2]
        self.fences[region] = (de, dd)

    def emit(self):
        nc = self.nc
        engs = {"pe": nc.tensor, "act": nc.scalar, "dve": nc.vector, "pool": nc.gpsimd, "sp": nc.sync}
        ops = self.ops
        for o in ops:
            for e, i in o.deps_eng.items():
                if e == "pe" and o.eng == "pe" and o.dma_key is None:
                    continue
                ops[i].needs_inc = True
        cnt = {e: 0 for e in engs}
        dma_cnt = {}
        for o in ops:
            if o.dma_key is not None:
                dma_cnt[o.dma_key] = dma_cnt.get(o.dma_key, 0) + 16
                o.sem_val = dma_cnt[o.dma_key]
            elif o.needs_inc:
                cnt[o.eng] += 1
                o.inc_val = cnt[o.eng]
        for o in ops:
            if o.dma_key in self.group_keys:
                o.sem_val = dma_cnt[o.dma_key]
        esem = {e: nc.alloc_semaphore("s_" + e) for e in engs}
        dsem = {}
        for k in dma_cnt:
            dsem[k] = nc.alloc_semaphore("d_" + "_".join(str(x) for x in k))
        waited = {}
        for o in ops:
            e = engs[o.eng]
            waits = {}
            for de, i in o.deps_eng.items():
                if de == "pe" and o.eng == "pe" and o.dma_key is None:
                    continue
                s = esem[de]
                v = ops[i].inc_val
                if waits.get(s, (0, 0))[1] < v:
                    waits[s] = (s, v)
            for i in o.deps_dma:
                s = dsem[ops[i].dma_key]
                v = ops[i].sem_val
                if waits.get(s, (0, 0))[1] < v:
                    waits[s] = (s, v)
            for s, v in waits.values():
                kk = (o.eng, s.num if hasattr(s, "num") else id(s))
                if waited.get(kk, 0) < v:
                    e.wait_ge(s, v)
                    waited[kk] = v
            ins = o.fn(e)
            if o.dma_key is not None:
                ins.then_inc(dsem[o.dma_key], 16)
            elif o.needs_inc:
                ins.then_inc(esem[o.eng], 1)
        return len(ops), len(dsem) + len(esem)


class Mem:
    def __init__(self, nc):
        self.nc = nc
        self.n = 0

    def at(self, name, shape, dtype, off):
        assert off % 32 == 0, (name, off)
        nbytes = int(np.prod(shape[1:])) * (2 if dtype == BF16 else 4)
        assert SB_BASE + off + nbytes <= SB_LIMIT, (name, off, nbytes)
        self.n += 1
        return self.nc.alloc_sbuf_tensor_at(f"{name}_{self.n}", list(shape), dtype, offset=SB_BASE + off).ap()


def build(dbg=()):
    nc = bass.Bass("TRN2", target_bir_lowering=False)
    R = Rec(nc)
    M = Mem(nc)
    dbg = set(dbg)
    dbg_outs = {}

    def din(name, shape):
        return nc.dram_tensor(name, list(shape), F32, kind="ExternalInput").ap()

    def dout(name, shape):
        return nc.dram_tensor(name, list(shape), F32, kind="ExternalOutput").ap()

    xc = din("xc", [T, D])
    cc = din("cc", [17, D])
    cache = din("cache", [16, 30, D])
    w_ada = din("w_ada", [D, 6 * D])
    b_ada = din("b_ada", [6, D])
    g_norm1 = din("g_norm1", [1, D])
    w_in = din("w_in", [D, 6 * D])
    b_in = din("b_in", [6, D])
    ln_v_g = din("ln_v_g", [1, D])
    ln_v_b = din("ln_v_b", [1, D])
    w_s = din("w_s", [4, 128, 128])
    b_s = din("b_s", [1, 512])
    w_pa = din("w_pa", [D, D])
    b_pa = din("b_pa", [1, D])
    conv_w = din("conv_w", [31, D])
    conv_b = din("conv_b", [1, D])
    ln_c_g = din("ln_c_g", [1, D])
    ln_c_b = din("ln_c_b", [1, D])
    w_pb = din("w_pb", [D, D])
    b_pb = din("b_pb", [1, D])
    w_o = din("w_o", [D, D])
    g_norm2 = din("g_norm2", [1, D])
    w_ffn_in = din("w_ffn_in", [D, 2 * DFF])
    w_ffn_out = din("w_ffn_out", [DFF, D])
    g_final = din("g_final", [1, D])

    y_out = dout("y", [T, D])
    ncp_out = dout("ncp", [30, D])
    ncs_out = dout("ncs", [16, 30, D])
    cvp_out = dout("cvp", [128, D])
    cvs_out = dout("cvs", [128, D])

    o = 0
    ident32 = M.at("ident32", [128, 128], F32, o); o += 512
    onesb = M.at("onesb", [128, 128], BF16, o); o += 256
    onesm = M.at("onesm", [128, 128], BF16, o); o += 256
    vecT = M.at("vecT", [128, KC, NVEC], F32, o); o += KC * NVEC * 4
    modT = M.at("modT", [128, 48, 17], F32, o); o += 48 * 17 * 4
    cT = M.at("cT", [128, KC, 17], BF16, o); o += 288
    WT = M.at("WT", [128, 4, 128], BF16, o); o += 1024
    WTS = M.at("WTS", [128, 4, 128], BF16, o); o += 1024
    ssum = M.at("ssum", [128, 64], F32, o); o += 256
    rstd = M.at("rstd", [128, 64], F32, o); o += 256
    st6 = M.at("st6", [128, 2, 2, 6], F32, o); o += 96
    mv = M.at("mv", [128, 2, 2], F32, o); o += 32
    epsT = M.at("epsT", [128, 1], F32, o); o += 32
    identb = M.at("identb", [128, 128], BF16, o); o += 256
    negh = M.at("negh", [128, 1], F32, o); o += 32
    o = (o + 255) // 256 * 256
    CONST_END = o
    WOFF = [o, o + 16384]
    o += 32768
    OFF_A = o; o += 34816
    OFF_C = o; o += 34816
    OFF_B = o; o += 34816
    OFF_D = o; o += 43008
    OFF_S = o
    SCR_SIZE = (SB_LIMIT - SB_BASE) - OFF_S
    assert SCR_SIZE >= 22528, SCR_SIZE

    Wb = [M.at(f"W{i}", [128, 8192], BF16, WOFF[i]) for i in range(2)]

    def wview(slot, kch, ncols):
        return Wb[slot][:, 0:kch * ncols].rearrange("p (k n) -> p k n", k=kch)

    PS = [nc.alloc_psum_tensor(f"ps{i}", [128, 1024], F32).ap() for i in range(4)]

    def bank(i):
        return PS[i // 2][:, (i % 2) * 512:(i % 2) * 512 + 512]

    hT = M.at("hT", [128, KC, T], BF16, OFF_A)
    nB = M.at("nB", [128, NTT, D], BF16, OFF_B)
    U = M.at("U", [128, KC, T], BF16, OFF_C)
    SA = M.at("SA", [128, KC, T], BF16, OFF_B)

    ST512 = [(0, 512), (512, 512), (1024, 512), (1536, 512), (2048, 128)]
    STMIX = [(0, 512), (512, 512), (1024, 384), (1408, 384), (1792, 384)]

    def tts(t0, w):
        return range(t0 // 128, (t0 + w + 127) // 128)

    def dma(eng, out, in_, reads, writes, key):
        R.op(eng, lambda e: e.dma_start(out=out, in_=in_), reads=reads, writes=writes, dma_key=key)

    def mm(out, lhsT, rhs, start, stop, reads, writes):
        R.op("pe", lambda e: e.matmul(out, lhsT=lhsT, rhs=rhs, start=start, stop=stop), reads=reads, writes=writes)

    def tp(out, in_, nrows, reads, writes):
        R.op("pe", lambda e: e.transpose(out, in_, ident32[0:nrows, 0:nrows]), reads=list(reads) + [("ident",)], writes=writes)

    def act(out, in_, func, reads, writes, bias=None, scale=None, accum_out=None):
        kw = {}
        if bias is not None:
            kw["bias"] = bias
        if scale is not None:
            kw["scale"] = scale
        if accum_out is not None:
            kw["accum_out"] = accum_out
        R.op("act", lambda e: e.activation(out=out, in_=in_, func=func, **kw), reads=reads, writes=writes)

    def tt_(eng, out, in0, in1, op, reads, writes):
        R.op(eng, lambda e: e.tensor_tensor(out=out, in0=in0, in1=in1, op=op), reads=reads, writes=writes)

    def ts_(eng, out, in0, s1, s2, op0, op1, reads, writes):
        if s2 is None:
            R.op(eng, lambda e: e.tensor_scalar(out=out, in0=in0, scalar1=s1, scalar2=None, op0=op0), reads=reads, writes=writes)
        else:
            R.op(eng, lambda e: e.tensor_scalar(out=out, in0=in0, scalar1=s1, scalar2=s2, op0=op0, op1=op1), reads=reads, writes=writes)

    def stt(out, in0, scalar, in1, op0, op1, reads, writes):
        R.op("dve", lambda e: e.scalar_tensor_tensor(out=out, in0=in0, scalar=scalar, in1=in1, op0=op0, op1=op1), reads=reads, writes=writes)

    def cp(eng, out, in_, reads, writes):
        R.op(eng, lambda e: e.tensor_copy(out=out, in_=in_), reads=reads, writes=writes)

    def dump(name, ap, shape, reads):
        if name not in dbg:
            return
        d = dout("dbg_" + name, shape)
        dbg_outs[name] = shape
        dma("sp" if ap.dtype == F32 else "pool", d, ap, reads, [("dbgout", name)], ("dbg", name))

    wstate = {"n": 0}

    def wload(src3, kch, ncols, extra_reads=()):
        slot = wstate["n"] % 2
        wstate["n"] += 1
        v = wview(slot, kch, ncols)
        dma("pool", v, src3, list(extra_reads), [("W", slot, 0), ("W", slot, 1)], ("W", slot, 0))
        return slot, v

    psrot = {"n": 0}

    def next_bank():
        b = psrot["n"] % 4
        psrot["n"] += 1
        return b

    def next_pair():
        psrot["n"] = (psrot["n"] + 1) // 2 * 2
        p = (psrot["n"] // 2) % 2
        psrot["n"] += 2
        return p

    R.op("pool", lambda e: e.memset(ident32, 1.0), writes=[("ident",)])
    R.op("pool", lambda e: e.affine_select(out=ident32, in_=ident32, pattern=[[-1, 128]], compare_op=ALU.is_equal,
                                           fill=0.0, base=0, channel_multiplier=1), reads=[("ident",)], writes=[("ident",)])
    R.op("dve", lambda e: e.memset(onesb, 1.0), writes=[("onesb",)])
    R.op("dve", lambda e: e.memset(onesm, 1.0 / 1024.0), writes=[("onesm",)])
    R.op("dve", lambda e: e.memset(epsT, EPS), writes=[("epsT",)])
    cp("dve", identb, ident32, [("ident",)], [("identb",)])
    R.op("dve", lambda e: e.memset(negh, -0.5), writes=[("negh",)])

    wa3 = w_ada.rearrange("(k p) n -> p k n", p=128)
    ADA = [M.at(f"ADA{i}", [128, KC, 1024], BF16, OFF_D + i * 16384) for i in range(2)]

    def ada_load(blk, extra_reads=(), buf=None):
        bi = blk % 2 if buf is None else buf
        dma("pool", ADA[bi], wa3[:, :, blk * 1024:(blk + 1) * 1024], list(extra_reads), [("D", "ada", bi)], ("ada", bi))

    def ada_compute(blk, buf=None, rot=False):
        bi = blk % 2 if buf is None else buf
        wv_ = ADA[bi]
        bk = next_bank() if rot else 4 + (blk % 2)
        pm = bank(bk)[:, 0:KC * 17].rearrange("p (j s) -> p j s", j=KC)
        for j in range(KC):
            for k in range(KC):
                mm(pm[:, j, :], wv_[:, k, j * 128:(j + 1) * 128], cT[:, k, :], k == 0, k == KC - 1,
                   [("D", "ada", bi), ("cT",)], [("ps", bk)])
        tt_("dve", modT[:, blk * 8:(blk + 1) * 8, :], pm, vecT[:, :, R_BADA + blk:R_BADA + blk + 1].to_broadcast([128, KC, 17]),
            ALU.add, [("ps", bk), ("vecT",)], [("modT", blk)])
        if blk in (M_G1, M_G2):
            gr = R_G1 if blk == M_G1 else R_G2
            ts_("dve", modT[:, blk * 8:(blk + 1) * 8, :], modT[:, blk * 8:(blk + 1) * 8, :], 1.0, None, ALU.add, None,
                [("modT", blk)], [("modT", blk)])
            tt_("dve", modT[:, blk * 8:(blk + 1) * 8, :], modT[:, blk * 8:(blk + 1) * 8, :],
                vecT[:, :, gr:gr + 1].to_broadcast([128, KC, 17]), ALU.mult, [("modT", blk), ("vecT",)], [("modT", blk)])

    BV2 = M.at("BV2", [2, D], BF16, OFF_D + 40960)
    bvH = BV2[0:1, :]
    dma("pool", bvH, b_in[1:2, :], [], [("E", "bvH")], ("m", 0))
    ada_load(0)
    ada_load(1)
    wi3 = w_in.rearrange("(k p) n -> p k n", p=128)
    slot_v, wv_v = wload(wi3[:, :, 1024:2048], KC, 1024, extra_reads=[("D", "ada", 1)])
    VEC = M.at("VEC", [NVEC, D], F32, OFF_S)
    CC = M.at("CC", [17, D], F32, OFF_S + 4096)
    WS32 = M.at("WS32", [128, 4, 128], F32, OFF_B)
    WTf = M.at("WTf", [128, 4, 128], F32, OFF_S + 14336)
    A8 = M.at("A8", [8, 4, 8], F32, OFF_B + 2048)
    Rm = M.at("Rm", [8, 128], F32, OFF_S + 16384 + 128)
    O1 = M.at("O1", [8, 4, 128], F32, OFF_S + 16384 + 640)
    R.group_keys.add(("setup",))
    R.group_keys.add(("setup2",))
    vec_srcs = [(R_BADA, 6, b_ada), (R_BIN, 6, b_in), (R_G1, 1, g_norm1), (R_LVG, 1, ln_v_g), (R_LVB, 1, ln_v_b),
                (R_BPA, 1, b_pa), (R_CB, 1, conv_b), (R_LCG, 1, ln_c_g), (R_LCB, 1, ln_c_b), (R_BPB, 1, b_pb),
                (R_G2, 1, g_norm2), (R_CW, 31, conv_w)]
    dma("sp", CC, cc, [], [("S", "CC")], ("setup",))
    for r0, nr, src in vec_srcs:
        dma("sp", VEC[r0:r0 + nr, :], src, [], [("S", "VEC", r0)], ("setup",))
    vec_keys = [("S", "VEC", r0) for r0, _, _ in vec_srcs]
    dma("sp", WS32, w_s.rearrange("g t s -> t g s"), [], [("B", 0)], ("setup2",))
    dma("sp", A8, w_s[:, 0:8, 0:8].rearrange("g b a -> b g a"), [], [("B", 1)], ("setup2",))

    pv = PS[3][:, 0:KC * NVEC].rearrange("p (c r) -> p c r", c=KC)
    for c in range(KC):
        tp(pv[:, c, :], VEC[:, c * 128:(c + 1) * 128], NVEC, vec_keys, [("ps", 6)])
    cp("dve", vecT, pv, [("ps", 6)], [("vecT",)])

    act(CC, CC, AF.Silu, [("S", "CC")], [("S", "CC")])
    pc = PS[3][:, 512:512 + KC * 17].rearrange("p (c r) -> p c r", c=KC)
    for c in range(KC):
        tp(pc[:, c, :], CC[:, c * 128:(c + 1) * 128], 17, [("S", "CC")], [("ps", 7)])
    cp("dve", cT, pc, [("ps", 7)], [("cT",)])
    ada_compute(0)
    ada_compute(1)

    dump("vecT", vecT, [128, KC, NVEC], [("vecT",)])
    dump("WT", WT, [128, 4, 128], [("WT",)], ) if False else None

    def modv(mi, tt):
        if tt < 16:
            return modT[:, mi * 8:(mi + 1) * 8, 16:17].to_broadcast([128, KC, 128])
        return modT[:, mi * 8:(mi + 1) * 8, 0:16].unsqueeze(3).to_broadcast([128, KC, 16, 8])

    def tokv(ap3, tt):
        if tt < 16:
            return ap3
        return ap3.rearrange("p k (q b) -> p k q b", b=8)

    def interleave(ga, na, gb, nb):
        ia = ib = 0
        ea = eb = False
        while not (ea and eb):
            if not ea and (eb or ia * nb <= ib * na):
                try:
                    next(ga)
                    ia += 1
                except StopIteration:
                    ea = True
            else:
                try:
                    next(gb)
                    ib += 1
                except StopIteration:
                    eb = True

    def drain(g):
        for _ in g:
            pass

    ncnt = {"n": 0}

    def newcol():
        ncnt["n"] += 1
        return ncnt["n"] % 64

    def rstd_from(col, src_ap, src_keys, scale):
        act(rstd[:, col:col + 1], src_ap, AF.Sqrt, list(src_keys) + [("epsT",)], [("rstd", col)], bias=epsT[:, 0:1], scale=scale)
        R.op("dve", lambda e: e.reciprocal(out=rstd[:, col:col + 1], in_=rstd[:, col:col + 1]), reads=[("rstd", col)], writes=[("rstd", col)])

    def rmsnorm_stages(src, src_keys, xn, kxn, sqj, pr, tt, dst_tile, dst_keys, mg, msh):
        col = newcol()
        pt = PS[pr].rearrange("p (c t) -> p c t", c=KC)
        pk = [("ps", 2 * pr), ("ps", 2 * pr + 1)]

        def sa():
            act(sqj, src, AF.Square, src_keys, [("S", "sqj"), ("ssum", col)], accum_out=ssum[:, col:col + 1])
            rstd_from(col, ssum[:, col:col + 1], [("ssum", col)], 1.0 / D)

        def sb():
            act(xn, src, AF.Identity, list(src_keys) + [("rstd", col)], [kxn], scale=rstd[:, col:col + 1])
            for c in range(KC):
                tp(pt[:, c, :], xn[:, c * 128:(c + 1) * 128], 128, [kxn], [("ps", 2 * pr + c // 4)])

        def sc():
            tt_("dve", tokv(pt, tt), tokv(pt, tt), modv(mg, tt), ALU.mult, pk + [("modT", mg)], pk)
            tt_("dve", tokv(dst_tile, tt), tokv(pt, tt), modv(msh, tt), ALU.add, pk + [("modT", msh)], dst_keys)
        return sa, sb, sc

    def skewed(stage_lists, pre=None, reverse=False, post=None):
        n = len(stage_lists[0])
        nt = len(stage_lists)
        for step in range(nt + n - 1):
            for si in (range(n - 1, -1, -1) if reverse else range(n)):
                t = step - si
                if 0 <= t < nt:
                    if si == 0 and pre is not None:
                        pre(t)
                    stage_lists[t][si]()
                    if post is not None and si == post[0]:
                        post[1](t)
            yield

    OUTK = []

    def hkey(k, tt):
        return ("A", tt)

    brot = {"n": 0}

    def proj_fm_gen(slot, wv, kch, in_buf, in_key, njc, evac, sts=STMIX, banks=None):
        for j in range(njc):
            for (t0, w) in sts:
                if banks is None:
                    b = next_bank()
                else:
                    b = banks[brot["n"] % len(banks)]
                    brot["n"] += 1
                pb = bank(b)[:, 0:w]
                for k in range(kch):
                    rk = [("W", slot, 0), ("W", slot, 1)] + [in_key(k, tt) for tt in tts(t0, w)]
                    mm(pb, wv[:, k, j * 128:(j + 1) * 128], in_buf[:, k, t0:t0 + w], k == 0, k == kch - 1, rk, [("ps", b)])
                evac(j, t0, w, pb, ("ps", b))
                yield

    def proj_fm(w3, kch, in_buf, in_key, njc, evac, sts=STMIX):
        slot, wv = wload(w3, kch, njc * 128)
        drain(proj_fm_gen(slot, wv, kch, in_buf, in_key, njc, evac, sts))

    LVGs = M.at("LVGs", [128, D], F32, OFF_D + 32768)
    LVBs = M.at("LVBs", [128, D], F32, OFF_D + 36864)
    bvL = M.at("bvL", [1, D], BF16, OFF_B + 32768)
    bv32 = M.at("bv32", [1, D], F32, OFF_B + 28672)
    dma("sp", bv32, b_in[1:2, :], [], [("E", "bv32")], ("m", 1))
    dma("sp", LVGs, ln_v_g.partition_broadcast(128), [], [("E", "LVG")], ("m", 2))
    dma("sp", LVBs, ln_v_b.partition_broadcast(128), [], [("E", "LVB")], ("m", 3))
    tt_("dve", bvL, bv32, bvH, ALU.subtract, [("E", "bv32"), ("E", "bvH")], [("E", "bvL")])
    dma("sp", BV2[1:2, :], bvL, [("E", "bvL")], [("E", "bvL2")], ("m", 4))

    R.fence("S")
    NXIN = 8
    XIN = [M.at(f"xin{i}", [128, D], F32, OFF_C + i * 4096) for i in range(NXIN)]
    XN = [M.at(f"xn{i}", [128, D], F32, OFF_S + 12288 + i * 4096) for i in range(2)]
    SQJ = M.at("sqj", [128, D], BF16, OFF_S + 20480)
    n1 = []
    for tt in range(NTT):
        s = tt % NXIN
        n1.append(rmsnorm_stages(XIN[s], [("C", "xin", s)], XN[tt % 2], ("S", "xn", tt % 2), SQJ, 2 + tt % 2, tt,
                                 hT[:, :, tt * 128:(tt + 1) * 128], [("A", tt)], M_G1, M_SH1))

    def pre_x(tt):
        s = tt % NXIN
        dma("sp", XIN[s], xc[tt * 128:(tt + 1) * 128, :], [("D", "ada", 1)] if 3 <= tt < NXIN else [], [("C", "xin", s)], ("xin", s))
    for tt in range(NXIN):
        pre_x(tt)

    def pre_x2(tt):
        if tt + NXIN < NTT and tt >= 0:
            pre_x(tt + NXIN)
    drain(skewed(n1, post=(1, pre_x2)))
    R.fence("C")
    slot_u, wv_u = wload(wi3[:, :, 0:1024], KC, 1024, extra_reads=[("A", 2)])
    ada_load(2, extra_reads=[("A", 13)])
    ada_load(3, extra_reads=[("A", 13)])
    dump("hT", hT, [128, KC, T], [("A", tt) for tt in range(NTT)])

    R.fence("S")
    VR = [M.at(f"vr{i}", [128, D], F32, OFF_S + i * 4096) for i in range(3)]
    LVG = PS[2]
    LVB = PS[3]
    cp("dve", LVG, LVGs, [("E", "LVG")], [("ps", 4), ("ps", 5)])
    cp("dve", LVB, LVBs, [("E", "LVB")], [("ps", 6), ("ps", 7)])

    def v_stages(tt):
        slot, wv = slot_v, wv_v
        vr = VR[tt % 3]
        kv = ("S", "vr", tt % 3)
        col = newcol()

        def sa():
            p = next_pair()
            pp = PS[p]
            pk = [("ps", 2 * p), ("ps", 2 * p + 1)]
            for h in range(2):
                o_ = pp[:, h * 512:(h + 1) * 512]
                for k in range(KC):
                    mm(o_, hT[:, k, tt * 128:(tt + 1) * 128], wv[:, k, h * 512:(h + 1) * 512], k == 0, False,
                       [("A", tt), ("W", slot, 0), ("W", slot, 1)], [pk[h]])
                mm(o_, onesb[0:2, :], BV2[0:2, h * 512:(h + 1) * 512], False, True, [("onesb",), ("E", "bvH"), ("E", "bvL2")], [pk[h]])
            act(vr, pp, AF.Gelu_apprx_tanh, pk, [kv])

        def sb():
            for h in range(2):
                R.op("dve", (lambda o2, i2: (lambda e: e.bn_stats(out=o2, in_=i2)))(st6[:, tt % 2, h, :], vr[:, h * 512:(h + 1) * 512]),
                     reads=[kv], writes=[("st6", tt % 2, h)])
            R.op("dve", (lambda o2, i2: (lambda e: e.bn_aggr(out=o2, in_=i2)))(mv[:, tt % 2, :], st6[:, tt % 2, :, :].rearrange("p h s -> p (h s)")),
                 reads=[("st6", tt % 2, 0), ("st6", tt % 2, 1)], writes=[("mv", tt % 2)])
            act(rstd[:, col:col + 1], mv[:, tt % 2, 1:2], AF.Sqrt, [("mv", tt % 2), ("epsT",)], [("rstd", col)], bias=epsT[:, 0:1], scale=1.0)

        def sc():
            R.op("dve", lambda e: e.reciprocal(out=rstd[:, col:col + 1], in_=rstd[:, col:col + 1]), reads=[("rstd", col)], writes=[("rstd", col)])
            ts_("dve", vr, vr, mv[:, tt % 2, 0:1], rstd[:, col:col + 1], ALU.subtract, ALU.mult, [kv, ("mv", tt % 2), ("rstd", col)], [kv])
            tt_("dve", vr, vr, LVG, ALU.mult, [kv, ("ps", 4), ("ps", 5)], [kv])
            if tt < 15:
                tt_("dve", nB[:, tt, :], vr, LVB, ALU.add, [kv, ("ps", 6), ("ps", 7)], [("B", tt)])
            else:
                tt_("dve", vr, vr, LVB, ALU.add, [kv, ("ps", 6), ("ps", 7)], [kv])
                cp("dve", nB[:, tt, :], vr, [kv], [("B", tt)])
                dma("sp", cvp_out if tt == 15 else cvs_out, vr, [kv], [("out", "cv", tt)], ("cvout", tt))
                OUTK.append(("out", "cv", tt))
        return sa, sb, sc

    def gen_v():
        return skewed([v_stages(tt) for tt in range(NTT)])

    def evac_u(j, t0, w, pb, bk):
        act(U[:, j, t0:t0 + w], pb, AF.Gelu_apprx_tanh, [bk, ("vecT",)], [("C", j, tt) for tt in tts(t0, w)],
            bias=vecT[:, j, R_BIN + 0:R_BIN + 1])
    bs32 = M.at("bs32", [1, 2, 512], F32, OFF_D + 32768)
    BS2 = M.at("BS2", [2, 2, 512], BF16, OFF_D + 36864)
    bsH = BS2[0:1, :, :]
    bsL = M.at("bsL", [1, 2, 512], BF16, OFF_D + 38912)

    def wt_stage1():
        R.op("pool", lambda e: e.memset(Rm, 1.0), writes=[("S", "Rm")])
        R.op("pool", lambda e: e.affine_select(out=Rm, in_=Rm, pattern=[[0, 16], [1, 8]], compare_op=ALU.is_equal, fill=0.0,
                                               base=0, channel_multiplier=-1), reads=[("S", "Rm")], writes=[("S", "Rm")])
        b1 = next_bank()
        pw = bank(b1).rearrange("p (g t) -> p g t", g=4)
        for g in range(4):
            tp(pw[:, g, :], WS32[:, g, :], 128, [("B", 0)], [("ps", b1)])
        cp("dve", WTf, pw, [("ps", b1)], [("S", "WTf")])
        R.op("pool", lambda e: e.affine_select(out=WT, in_=WTf, pattern=[[0, 4], [1, 128]], compare_op=ALU.is_ge, fill=0.0,
                                               base=0, channel_multiplier=-1), reads=[("S", "WTf")], writes=[("WT",)])
        b2 = next_bank()
        po1 = bank(b2)[0:8, :].rearrange("p (g t) -> p g t", g=4)
        for g in range(4):
            mm(po1[:, g, :], A8[:, g, :], Rm, True, True, [("B", 1), ("S", "Rm")], [("ps", b2)])
        cp("dve", O1, po1, [("ps", b2)], [("S", "O1")])
        dma("sp", bs32[:, 0, :], b_s, [], [("S", "bs32"), ("E", "LVG")], ("m", 1))

    def wt_stage2():
        b1 = next_bank()
        pw = bank(b1).rearrange("p (g t) -> p g t", g=4)
        for g in range(4):
            mm(pw[:, g, :], Rm, O1[:, g, :], True, True, [("S", "O1"), ("S", "Rm")], [("ps", b1)])
        cp("dve", WTf, pw, [("ps", b1)], [("S", "WTf")])
        R.op("pool", lambda e: e.affine_select(out=WTf, in_=WTf, pattern=[[0, 4], [8, 16], [1, 8]], compare_op=ALU.is_ge, fill=0.0,
                                               base=0, channel_multiplier=-1), reads=[("S", "WTf")], writes=[("S", "WTf")])
        R.op("pool", lambda e: e.affine_select(out=WTS, in_=WTf, pattern=[[0, 4], [-8, 16], [0, 8]], compare_op=ALU.is_ge, fill=0.0,
                                               base=0, channel_multiplier=1), reads=[("S", "WTf")], writes=[("WTS",)])
        cp("dve", bs32[:, 1, :].rearrange("p (g q b) -> p g q b", g=4, q=16),
           bs32[:, 0, :].rearrange("p (g t) -> p g t", g=4)[:, :, 0:8].unsqueeze(2).to_broadcast([1, 4, 16, 8]),
           [("S", "bs32")], [("S", "bs32b")])
        cp("dve", bsH, bs32, [("S", "bs32"), ("S", "bs32b")], [("S", "bsH"), ("E", "LVB")])
        tt_("dve", bsL, bs32, bsH, ALU.subtract, [("S", "bs32"), ("S", "bs32b"), ("S", "bsH")], [("S", "bsL"), ("E", "LVB")])
        dma("sp", BS2[1:2, :, :], bsL, [("S", "bsL")], [("S", "bsL2")], ("m", 4))

    wt_stage1()
    vgen = gen_v()
    ugen = proj_fm_gen(slot_u, wv_u, KC, hT, hkey, 8, evac_u)
    next(vgen)
    next(ugen)
    next(vgen)
    next(ugen)
    wt_stage2()
    for _ in range(4):
        next(vgen)
        next(ugen)
    ada_compute(2, rot=True)
    ada_compute(3, rot=True)
    wv_ga = ADA[0]
    dma("pool", wv_ga, wi3[:, :, 4096:5120], [], [("D", "ada", 0)], ("ada", 0))
    ada_load(4, buf=1)
    interleave(vgen, NTT - 4, ugen, 20)
    dump("n", nB, [128, NTT, D], [("B", tt) for tt in range(NTT)])
    dump("U", U, [128, KC, T], [("C", j, tt) for j in range(KC) for tt in range(NTT)])

    R.fence("S")

    def gen_mix():
        for tt in range(NTT):
            gi = 0 if tt < 16 else 1
            Wm, wk = (WT, ("WT",)) if tt < 16 else (WTS, ("WTS",))
            for jg in range(2):
                b = next_bank()
                pb = bank(b).rearrange("p (j t) -> p j t", j=4)
                for jj in range(4):
                    j = jg * 4 + jj
                    g = j // 2
                    mm(pb[:, jj, :], nB[:, tt, j * 128:(j + 1) * 128], Wm[:, g, :], True, False, [("B", tt), wk], [("ps", b)])
                    mm(pb[:, jj, :], onesb[0:2, :], BS2[0:2, gi, g * 128:(g + 1) * 128], False, True, [("onesb",), ("S", "bsH"), ("S", "bsL2")], [("ps", b)])
                uk = [("C", j, tt) for j in range(jg * 4, jg * 4 + 4)]
                uv = U[:, jg * 4:(jg + 1) * 4, tt * 128:(tt + 1) * 128]
                tt_("dve", uv, pb, uv, ALU.mult, [("ps", b)] + uk, uk)
            yield
    ada_compute(4, buf=1)
    ada_load(5, buf=1)
    ga_jobs = [(j, t0, w) for j in range(KC) for (t0, w) in STMIX]

    def ga_alias(j, t0, w):
        return (j * 4352 + 2 * t0) // 2048, (j * 4352 + 2 * (t0 + w) - 1) // 2048

    def emit_ga(j, t0, w):
        b = next_bank()
        pb = bank(b)[:, 0:w]
        for k in range(KC):
            rk = [("D", "ada", 0)] + [("A", tt) for tt in tts(t0, w)]
            mm(pb, wv_ga[:, k, j * 128:(j + 1) * 128], hT[:, k, t0:t0 + w], k == 0, k == KC - 1, rk, [("ps", b)])
        lo, hi = ga_alias(j, t0, w)
        act(SA[:, j, t0:t0 + w], pb, AF.Sigmoid, [("ps", b), ("vecT",)],
            [("B", j, tt) for tt in tts(t0, w)] + [("B", t2) for t2 in range(lo, hi + 1)], bias=vecT[:, j, R_BIN + 4:R_BIN + 5])

    mixg = gen_mix()
    gi_ = 0
    for t in range(NTT):
        next(mixg)
        for _ in range(3):
            if gi_ < len(ga_jobs) and ga_alias(*ga_jobs[gi_])[1] <= t:
                emit_ga(*ga_jobs[gi_])
                gi_ += 1
    while gi_ < len(ga_jobs):
        emit_ga(*ga_jobs[gi_])
        gi_ += 1
    dump("ug", U, [128, KC, T], [("C", j, tt) for j in range(KC) for tt in range(NTT)])

    def evac_sig(dst, reg, brow):
        def f(j, t0, w, pb, bk):
            act(dst[:, j, t0:t0 + w], pb, AF.Sigmoid, [bk, ("vecT",)], [(reg, j, tt) for tt in tts(t0, w)],
                bias=vecT[:, j, brow:brow + 1])
        return f
    ada_compute(5, buf=1)
    dump("modT", modT, [128, 48, 17], [("modT", i) for i in range(6)])

    def evac_ma(j, t0, w, pb, bk):
        ks = [("B", j, tt) for tt in tts(t0, w)]
        stt(SA[:, j, t0:t0 + w], pb, vecT[:, j, R_BPA:R_BPA + 1], SA[:, j, t0:t0 + w], ALU.add, ALU.mult, [bk, ("vecT",)] + ks, ks)
    proj_fm(w_pa.rearrange("(k p) n -> p k n", p=128), KC, U, lambda k, tt: ("C", k, tt), 8, evac_ma)
    dump("MA", SA, [128, KC, T], [("B", j, tt) for j in range(KC) for tt in range(NTT)])

    R.fence("C")
    R.fence("S")
    R.fence("D")
    R.fence("E")
    SGB = U
    proj_fm(wi3[:, :, 3072:4096], KC, hT, hkey, 8, evac_sig(SGB, "C", R_BIN + 3))
    GP = M.at("GP", [128, KC, 2080], BF16, OFF_D)
    GS = M.at("GS", [128, KC, 16, 38], BF16, OFF_D + 33280)
    GF32 = M.at("GF32", [128, KC, 160], F32, OFF_S)
    CA = [M.at(f"CA{i}", [128, D], F32, OFF_S + 5120 + i * 4096) for i in range(2)]
    STG = [M.at(f"STG{i}", [128, D], F32, OFF_S + 13312 + i * 4096) for i in range(2)]
    R.op("dve", lambda e: e.memset(GP[:, :, 0:30], 0.0), writes=[("D", "hist")])
    for gq in range(4):
        s_ = gq % 2
        dma("sp", CA[s_][0:120, :], cache[gq * 4:(gq + 1) * 4].rearrange("q r d -> (q r) d"), [], [("S", "ca", s_)], ("ca", s_))
        pr = 2 + (gq % 2)
        pt = PS[pr].rearrange("p (c t) -> p c t", c=KC)
        for c in range(KC):
            tp(pt[:, c, 0:120], CA[s_][0:120, c * 128:(c + 1) * 128], 120, [("S", "ca", s_)], [("ps", 2 * pr + c // 4)])
        cp("dve", GS[:, :, gq * 4:(gq + 1) * 4, 0:30], pt[:, :, 0:120].rearrange("p c (q r) -> p c q r", q=4),
           [("ps", 2 * pr), ("ps", 2 * pr + 1)], [("D", "gsh", gq)])

    def evac_glu(j, t0, w, pb, bk):
        bga = vecT[:, j, R_BIN + 2:R_BIN + 3]
        ck = [("C", j, tt) for tt in tts(t0, w)]
        if t0 < 2048:
            stt(GP[:, j, 30 + t0:30 + t0 + w], pb, bga, SGB[:, j, t0:t0 + w], ALU.add, ALU.mult, [bk, ("vecT",)] + ck,
                [("D", j, tt) for tt in tts(t0, w)])
            if t0 == 1536:
                stt(GF32[:, j, 0:32], pb[:, 480:512], bga, SGB[:, j, 2016:2048], ALU.add, ALU.mult, [bk, ("vecT",)] + ck, [("S", "gf", j, 0)])
        else:
            stt(GS[:, j, :, 30:38], pb.rearrange("p (q b) -> p q b", b=8), bga, SGB[:, j, 2048:2176].rearrange("p (q b) -> p q b", b=8),
                ALU.add, ALU.mult, [bk, ("vecT",)] + ck, [("D", j, 16)])
            stt(GF32[:, j, 32:160], pb, bga, SGB[:, j, 2048:2176], ALU.add, ALU.mult, [bk, ("vecT",)] + ck, [("S", "gf", j, 1)])
    proj_fm(wi3[:, :, 2048:3072], KC, hT, hkey, 8, evac_glu, sts=ST512)
    pt = PS[2].rearrange("p (c t) -> p c t", c=KC)
    for j in range(KC):
        tp(pt[0:30, j, :], GF32[:, j, 2:32], 128, [("S", "gf", j, 0)], [("ps", 4 + j // 4)])
    cp("dve", STG[0][0:30, :], PS[2][0:30, :], [("ps", 4), ("ps", 5)], [("S", "stg", 0)])
    dma("sp", ncp_out, STG[0][0:30, :], [("S", "stg", 0)], [("out", "ncp")], ("ncp",))
    OUTK.append(("out", "ncp"))
    pt = PS[3].rearrange("p (c t) -> p c t", c=KC)
    for j in range(KC):
        tp(pt[:, j, :], GF32[:, j, 32:160], 128, [("S", "gf", j, 1)], [("ps", 6 + j // 4)])
    cp("dve", STG[1], PS[3], [("ps", 6), ("ps", 7)], [("S", "stg", 1)])
    R.group_keys.add(("ncs",))
    for q in range(16):
        dma("sp", ncs_out[q, 22:30, :], STG[1][q * 8:(q + 1) * 8, :], [("S", "stg", 1)], [("out", "ncs", q)], ("ncs",))
        OUTK.append(("out", "ncs", q))
    dma("sp", ncs_out[:, 0:22, :], cache[:, 8:30, :], [], [("out", "ncs0")], ("ncs",))
    OUTK.append(("out", "ncs0"))
    dump("GP", GP[:, :, 0:2078], [128, KC, 2078], [("D", "hist")] + [("D", j, tt) for j in range(KC) for tt in range(16)])
    dump("GS", GS, [128, KC, 16, 38], [("D", "gsh", g) for g in range(4)] + [("D", j, 16) for j in range(KC)])

    R.fence("C")
    SB = U
    proj_fm(wi3[:, :, 5120:6144], KC, hT, hkey, 8, evac_sig(SB, "C", R_BIN + 5))

    R.fence("S")
    R.fence("A")
    CV = M.at("CV", [128, KC, T], BF16, OFF_A)
    DG = [M.at(f"DG{i}", [128, 31, 128], BF16, OFF_S + i * 7936) for i in range(2)]
    slot_pb, wv_pb = wload(w_pb.rearrange("(k p) n -> p k n", p=128), KC, 1024)
    NPE = 26

    def build_dg(c):
        tt_("dve", DG[c % 2], identb.unsqueeze(1).to_broadcast([128, 31, 128]),
            vecT[:, c, R_CW:R_CW + 31].unsqueeze(2).to_broadcast([128, 31, 128]), ALU.mult, [("identb",), ("vecT",)], [("S", "dg", c % 2)])
    build_dg(0)
    build_dg(1)
    for c in range(KC):
        dg = DG[c % 2]
        kd = ("S", "dg", c % 2)
        for (t0, w) in ST512:
            b = next_bank()
            pb = bank(b)[:, 0:w]
            if t0 < 2048:
                rk = [kd] + [("D", c, tt) for tt in range(max(0, (t0 - 30) // 128), (t0 + w - 1) // 128 + 1)]
                if t0 == 0:
                    rk.append(("D", "hist"))
                for k in range(NPE):
                    mm(pb, dg[:, k, :], GP[:, c, t0 + k:t0 + k + w], k == 0, k == NPE - 1, rk, [("ps", b)])
                for k in range(NPE, 31):
                    stt(pb, GP[:, c, t0 + k:t0 + k + w], vecT[:, c, R_CW + k:R_CW + k + 1], pb, ALU.mult, ALU.add,
                        rk[1:] + [("ps", b), ("vecT",)], [("ps", b)])
            else:
                rk = [kd, ("D", c, 16)] + [("D", "gsh", g) for g in range(4)]
                for k in range(NPE):
                    mm(pb, dg[:, k, :], GS[:, c, :, k:k + 8], k == 0, k == NPE - 1, rk, [("ps", b)])
                pb3 = pb.rearrange("p (q b) -> p q b", b=8)
                for k in range(NPE, 31):
                    stt(pb3, GS[:, c, :, k:k + 8], vecT[:, c, R_CW + k:R_CW + k + 1], pb3, ALU.mult, ALU.add,
                        rk[1:] + [("ps", b), ("vecT",)], [("ps", b)])
            act(CV[:, c, t0:t0 + w], pb, AF.Identity, [("ps", b), ("vecT",)], [("A", c, tt) for tt in tts(t0, w)],
                bias=vecT[:, c, R_CB:R_CB + 1])
        if c + 2 < KC:
            build_dg(c + 2)
    dump("CV", CV, [128, KC, T], [("A", j, tt) for j in range(KC) for tt in range(NTT)])

    R.fence("S")
    NF = [M.at(f"NF{i}", [128, D], F32, OFF_S + i * 4096) for i in range(2)]
    VEPS = M.at("veps", [128, 32], F32, OFF_S + 8192)

    def ln_stages(tt):
        pa = PS[2]
        pa3 = pa.rearrange("p (c t) -> p c t", c=KC)
        ka = [("ps", 4), ("ps", 5)]
        pcp = 3 if tt % 2 == 0 else 1
        pc = PS[pcp].rearrange("p (c t) -> p c t", c=KC)
        kc_ = [("ps", 2 * pcp), ("ps", 2 * pcp + 1)]
        nf = NF[tt % 2]
        knf = ("S", "nf", tt % 2)
        col = newcol()

        def sa():
            for c in range(KC):
                mm(pa3[:, c, :], CV[:, c, tt * 128:(tt + 1) * 128], identb, True, True, [("A", c, tt), ("identb",)], [ka[c // 4]])
            act(nf, pa, AF.Identity, ka, [knf])

        def sb():
            for h in range(2):
                R.op("dve", (lambda o2, i2: (lambda e: e.bn_stats(out=o2, in_=i2)))(st6[:, tt % 2, h, :], nf[:, h * 512:(h + 1) * 512]),
                     reads=[knf], writes=[("st6", tt % 2, h)])
            R.op("dve", (lambda o2, i2: (lambda e: e.bn_aggr(out=o2, in_=i2)))(mv[:, tt % 2, :], st6[:, tt % 2, :, :].rearrange("p h s -> p (h s)")),
                 reads=[("st6", tt % 2, 0), ("st6", tt % 2, 1)], writes=[("mv", tt % 2)])
            ts_("dve", VEPS[:, tt:tt + 1], mv[:, tt % 2, 1:2], EPS, None, ALU.add, None, [("mv", tt % 2)], [("S", "veps", tt)])
            tt_("pool", rstd[:, col:col + 1], VEPS[:, tt:tt + 1], negh, ALU.pow, [("S", "veps", tt), ("negh",)], [("rstd", col)])

        def sc():
            ts_("dve", nf, nf, mv[:, tt % 2, 0:1], rstd[:, col:col + 1], ALU.subtract, ALU.mult, [knf, ("mv", tt % 2), ("rstd", col)], [knf])
            for c in range(KC):
                tp(pc[:, c, :], nf[:, c * 128:(c + 1) * 128], 128, [knf], [kc_[c // 4]])

        def sd():
            for c in range(KC):
                act(CV[:, c, tt * 128:(tt + 1) * 128], pc[:, c, :], AF.Silu, [kc_[c // 4], ("vecT",)], [("A", c, tt)],
                    bias=vecT[:, c, R_LCB:R_LCB + 1], scale=vecT[:, c, R_LCG:R_LCG + 1])
        return sa, sb, sc, sd

    def evac_m(j, t0, w, pb, bk):
        kb = [("B", j, tt) for tt in tts(t0, w)]
        kc = [("C", j, tt) for tt in tts(t0, w)]
        stt(pb, pb, vecT[:, j, R_BPB:R_BPB + 1], SB[:, j, t0:t0 + w], ALU.add, ALU.mult, [bk, ("vecT",)] + kc, [bk])
        tt_("dve", SA[:, j, t0:t0 + w], pb, SA[:, j, t0:t0 + w], ALU.add, [bk] + kb, kb)

    ckey = lambda k, tt: ("A", k, tt)
    lng = skewed([ln_stages(tt) for tt in range(NTT)], reverse=True)
    pend = []
    done_st = 0
    jb = {"n": 0}
    for step in range(NTT + 3 + 64):
        cur = []
        for _ in range(2):
            if pend:
                j, t0, w = pend.pop(0)
                b = jb["n"] % 2
                jb["n"] += 1
                pb = bank(b)[:, 0:w]
                for k in range(KC):
                    rk = [("W", slot_pb, 0), ("W", slot_pb, 1)] + [("A", k, tt) for tt in tts(t0, w)]
                    mm(pb, wv_pb[:, k, j * 128:(j + 1) * 128], CV[:, k, t0:t0 + w], k == 0, k == KC - 1, rk, [("ps", b)])
                cur.append((j, t0, w, pb, ("ps", b)))
        alive = next(lng, "end") != "end"
        ndone = step - 3 + 1
        while done_st < len(STMIX) and ndone * 128 >= STMIX[done_st][0] + STMIX[done_st][1]:
            pend.extend((j, STMIX[done_st][0], STMIX[done_st][1]) for j in range(KC))
            done_st += 1
        for (j, t0, w, pb, bk) in cur:
            evac_m(j, t0, w, pb, bk)
        if not alive and not pend and done_st == len(STMIX):
            break
    dump("S", CV, [128, KC, T], [("A", j, tt) for j in range(KC) for tt in range(NTT)])
    dump("M", SA, [128, KC, T], [("B", j, tt) for j in range(KC) for tt in range(NTT)])

    def build_gt(mi, dst, dkey, gx, kgx):
        for gi in range(2):
            ttx = 0 if gi == 0 else 16
            cp("dve", tokv(gx, ttx), modv(mi, ttx), [("modT", mi)], [kgx])
            pr = 2 + gi
            pt_ = PS[pr].rearrange("p (c t) -> p c t", c=KC)
            for c in range(KC):
                tp(pt_[:, c, :], gx[:, c, :], 128, [kgx], [("ps", 2 * pr + c // 4)])
            cp("dve", dst[:, gi, :], PS[pr], [("ps", 2 * pr), ("ps", 2 * pr + 1)], [(dkey[0], dkey[1], gi)])

    R.fence("S")
    R.fence("D")
    XIN2 = [M.at(f"xin2_{i}", [128, D], F32, OFF_S + i * 4096) for i in range(3)]
    XNB = [M.at(f"xnb{i}", [128, D], F32, OFF_S + 12288 + i * 4096) for i in range(2)]
    SQJ2 = M.at("sqj2", [128, D], BF16, OFF_S + 20480)
    GXv = XNB[0].rearrange("p (c t) -> p c t", c=KC)
    GT1 = M.at("GT1", [128, 2, D], F32, OFF_D)
    h2T = M.at("h2T", [128, KC, 1152], BF16, OFF_B + 59392)
    ACTB = M.at("ACTB", [128, FC, 1152], BF16, OFF_B)
    GT2 = M.at("GT2", [128, 2, D], F32, OFF_B + 50688)
    build_gt(M_GT1, GT1, ("D", "gt1"), GXv, ("S", "xn", 0))
    R.fence_into("X", ["A", "C"])
    X1 = M.at("X1", [128, NTT, D], F32, OFF_A)
    GROUPS = (list(range(0, 9)), list(range(9, 17)))

    def norm2_stages(i, tt):
        return rmsnorm_stages(X1[:, tt, :], [("X", tt, q) for q in range(4)], XNB[i % 2], ("S", "xn", i % 2), SQJ2, 2 + i % 2, tt,
                              h2T[:, :, i * 128:(i + 1) * 128], [("D", "h", i)], M_G2, M_SH2)

    slot, wv = wload(w_o.rearrange("(k p) n -> p k n", p=128), KC, 1024)
    for tt in range(NTT):
        s_ = tt % 3
        dma("sp", XIN2[s_], xc[tt * 128:(tt + 1) * 128, :], [], [("S", "xin", s_)], ("xin", s_))
        p = next_pair()
        pp = PS[p]
        pk = [("ps", 2 * p), ("ps", 2 * p + 1)]
        for h in range(2):
            for k in range(KC):
                mm(pp[:, h * 512:(h + 1) * 512], SA[:, k, tt * 128:(tt + 1) * 128], wv[:, k, h * 512:(h + 1) * 512], k == 0, k == KC - 1,
                   [("B", k, tt), ("W", slot, 0), ("W", slot, 1)], [pk[h]])
        gi = 0 if tt < 16 else 1
        for h in range(2):
            hs = slice(h * 512, (h + 1) * 512)
            tt_("dve", pp[:, hs], pp[:, hs], GT1[:, gi, hs], ALU.mult, [pk[h], ("D", "gt1", gi)], [pk[h]])
        for h in range(2):
            hs = slice(h * 512, (h + 1) * 512)
            tt_("dve", X1[:, tt, hs], pp[:, hs], XIN2[s_][:, hs], ALU.add, [pk[h], ("S", "xin", s_)], [("X", tt, 2 * h), ("X", tt, 2 * h + 1)])
        if tt == 1:
            g0gen = skewed([norm2_stages(i, t_) for i, t_ in enumerate(GROUPS[0])])
        nst_ = len(GROUPS[0]) + 2
        if tt >= 1 and (tt * nst_) // 16 > ((tt - 1) * nst_) // 16:
            next(g0gen, None)
    drain(g0gen)
    dump("X1", X1, [128, NTT, D], [("X", tt, q) for tt in range(NTT) for q in range(4)])

    R.fence_into("F", ["B", "D"])
    R.fence("S")
    SG = [M.at(f"SG{i}", [128, 512], F32, OFF_S + i * 2048) for i in range(2)]
    GFB = M.at("GFB", [128, D], F32, OFF_S + 4096)
    dma("sp", GFB, g_final.partition_broadcast(128), [], [("S", "gfb")], ("m", 2))
    build_gt(M_GT2, GT2, ("F", "gt2"), GXv, ("S", "xn", 0))
    wf3 = w_ffn_in.rearrange("(k p) n -> p k n", p=128)
    wo3 = w_ffn_out.rearrange("(j p) n -> p j n", p=128)
    R.group_keys.add(("yout",))
    sgc = {"n": 0}

    gfb = {"ap": GFB, "keys": [("S", "gfb")]}

    def final_tile(tt):
        col = newcol()
        xk = [("X", tt, q) for q in range(4)]
        act(SQJ2, X1[:, tt, :], AF.Square, xk, [("S", "sqj"), ("ssum", col)], accum_out=ssum[:, col:col + 1])
        act(rstd[:, col:col + 1], ssum[:, col:col + 1], AF.Sqrt, [("ssum", col), ("epsT",)], [("rstd", col)], bias=epsT[:, 0:1], scale=1.0 / D)

        def part_b():
            R.op("dve", lambda e: e.reciprocal(out=rstd[:, col:col + 1], in_=rstd[:, col:col + 1]), reads=[("rstd", col)], writes=[("rstd", col)])
            stt(X1[:, tt, :], X1[:, tt, :], rstd[:, col:col + 1], gfb["ap"], ALU.mult, ALU.mult, xk + [("rstd", col)] + gfb["keys"], xk)
            dma("sp", y_out[tt * 128:(tt + 1) * 128, :], X1[:, tt, :], xk, [("out", "y", tt)], ("yout",))
            OUTK.append(("out", "y", tt))
        return part_b

    def gen_ffn_in(tiles):
        nloc = len(tiles) * 128
        lst = [(0, 384), (384, 384), (768, 384)] if nloc == 1152 else [(l0, min(512, nloc - l0)) for l0 in range(0, nloc, 512)]
        for wl in range(6):
            j0 = wl * 4
            nj = min(4, FC - j0)
            slot = wstate["n"] % 2
            wstate["n"] += 1
            wv = wview(slot, KC, 1024)
            dma("pool", wv[:, :, 0:nj * 128], wf3[:, :, j0 * 128:(j0 + nj) * 128], [], [("W", slot, 0)], ("W", slot, 0))
            dma("pool", wv[:, :, 512:512 + nj * 128], wf3[:, :, DFF + j0 * 128:DFF + (j0 + nj) * 128], [], [("W", slot, 1)], ("W", slot, 1))
            for jj in range(nj):
                j = j0 + jj
                for (l0, w) in lst:
                    bg = next_bank()
                    bu = next_bank()
                    hk = [("D", "h", i) for i in tts(l0, w)]
                    for k in range(KC):
                        mm(bank(bg)[:, 0:w], wv[:, k, jj * 128:(jj + 1) * 128], h2T[:, k, l0:l0 + w], k == 0, k == KC - 1,
                           [("W", slot, 0)] + hk, [("ps", bg)])
                    for k in range(KC):
                        mm(bank(bu)[:, 0:w], wv[:, k, 512 + jj * 128:512 + (jj + 1) * 128], h2T[:, k, l0:l0 + w], k == 0, k == KC - 1,
                           [("W", slot, 1)] + hk, [("ps", bu)])
                    sg = SG[sgc["n"] % 2]
                    ksg = ("S", "sg", sgc["n"] % 2)
                    sgc["n"] += 1
                    act(sg[:, 0:w], bank(bg)[:, 0:w], AF.Silu, [("ps", bg)], [ksg])
                    tt_("dve", ACTB[:, j, l0:l0 + w], bank(bu)[:, 0:w], sg[:, 0:w], ALU.mult, [("ps", bu), ksg],
                        [("F", "a", j, i) for i in tts(l0, w)])
                    yield

    defer = [None]

    def gen_ffn_out(tiles):
        for q in range(4):
            slot, wv = wload(wo3[:, :, q * 256:(q + 1) * 256], FC, 256)
            for i, tt in enumerate(tiles):
                b = next_bank()
                pb = bank(b)[:, 0:256]
                for j in range(FC):
                    mm(pb, ACTB[:, j, i * 128:(i + 1) * 128], wv[:, j, :], j == 0, j == FC - 1,
                       [("F", "a", j, i), ("W", slot, 0), ("W", slot, 1)], [("ps", b)])
                gi = 0 if tt < 16 else 1
                xq = X1[:, tt, q * 256:(q + 1) * 256]
                tt_("dve", pb, pb, GT2[:, gi, q * 256:(q + 1) * 256], ALU.mult, [("ps", b), ("F", "gt2", gi)], [("ps", b)])
                tt_("dve", xq, pb, xq, ALU.add, [("ps", b), ("X", tt, q)], [("X", tt, q)])
                if q == 3:
                    pb_new = final_tile(tt)
                    if defer[0] is not None:
                        defer[0]()
                    defer[0] = pb_new
                yield
        if defer[0] is not None:
            defer[0]()
            defer[0] = None

    def gen_norm2(tiles):
        return skewed([norm2_stages(i, t_) for i, t_ in enumerate(tiles)])

    drain(gen_ffn_in(GROUPS[0]))
    interleave(gen_ffn_out(GROUPS[0]), 4 * len(GROUPS[0]), gen_norm2(GROUPS[1]), 12)
    cp("dve", PS[3], GFB, [("S", "gfb")], [("ps", 6), ("ps", 7)])
    gfb["ap"] = PS[3]
    gfb["keys"] = [("ps", 6), ("ps", 7)]
    drain(gen_ffn_in(GROUPS[1]))
    drain(gen_ffn_out(GROUPS[1]))

    out_keys = [("dbgout", n) for n in dbg_outs] + OUTK
    R.op("sp", lambda e: e.nop(), reads=out_keys, writes=[])
    nops, nsem = R.emit()
    return nc, dbg_outs, (nops, nsem)


_CACHE = {}


def _prep_inputs(inputs, b):
    f = lambda a: np.ascontiguousarray(np.asarray(a, dtype=np.float32))
    xs = f(inputs["x_sample"])[16 * b:16 * b + 16].reshape(128, D)
    m = {
        "xc": np.concatenate([f(inputs["x_prompt"])[b], xs], axis=0),
        "cc": np.concatenate([f(inputs["c_sample"])[16 * b:16 * b + 16], f(inputs["c_prompt"])[b:b + 1]], axis=0),
        "cache": f(inputs["cache_conv"])[0, 16 * b:16 * b + 16],
        "w_ada": f(inputs["w_ada"])[0], "b_ada": f(inputs["b_ada"])[0].reshape(6, D),
        "g_norm1": f(inputs["g_norm1"]), "w_in": f(inputs["w_in"])[0], "b_in": f(inputs["b_in"])[0].reshape(6, D),
        "ln_v_g": f(inputs["ln_v_g"]), "ln_v_b": f(inputs["ln_v_b"]),
        "w_s": f(inputs["w_s"])[0], "b_s": f(inputs["b_s"])[0].reshape(1, 512),
        "w_pa": f(inputs["w_pa"])[0], "b_pa": f(inputs["b_pa"]),
        "conv_w": f(inputs["conv_w"])[0], "conv_b": f(inputs["conv_b"]),
        "ln_c_g": f(inputs["ln_c_g"]), "ln_c_b": f(inputs["ln_c_b"]),
        "w_pb": f(inputs["w_pb"])[0], "b_pb": f(inputs["b_pb"]),
        "w_o": f(inputs["w_o"])[0], "g_norm2": f(inputs["g_norm2"]),
        "w_ffn_in": f(inputs["w_ffn_in"])[0], "w_ffn_out": f(inputs["w_ffn_out"])[0],
        "g_final": f(inputs["g_final"]).reshape(1, D),
    }
    return {k: np.ascontiguousarray(v) for k, v in m.items()}


def kernel(**inputs):
    if "nc" not in _CACHE:
        _CACHE["nc"] = build()[0]
    nc = _CACHE["nc"]
    in_maps = [_prep_inputs(inputs, b) for b in range(NCORES)]
    res = run_bass_kernel_spmd(nc, in_maps, core_ids=list(range(NCORES)))
    rs = res.results
    y_prompt = np.stack([r["y"][:2048] for r in rs]).astype(np.float32)
    y_sample = np.concatenate([r["y"][2048:].reshape(16, 8, D) for r in rs]).astype(np.float32)
    ncp = np.stack([r["ncp"] for r in rs])[None].astype(np.float32)
    ncs = np.concatenate([r["ncs"] for r in rs])[None].astype(np.float32)
    cvp = np.stack([r["cvp"] for r in rs])[None].astype(np.float32)
    cvs = np.concatenate([r["cvs"].reshape(16, 8, D) for r in rs])[None].astype(np.float32)
    return (y_prompt, y_sample, ncp, ncs, cvp, cvs)
```
